# Optimizing a Trainium2 kernel written in Bass

```python
import jax, jax.numpy as jnp
from jax import lax
import numpy as np

D_MODEL = 2048
BATCH = 2
SEQ = 8192
DEPTH = 2
DEC_BATCH = 16
DEC_SEQ = 32
PAST_LEN = 2048

CHUNK = 64
N_EVEN = (DEPTH + 1) // 2
N_ODD = DEPTH // 2
EPS = 1e-6
A_WIDTH = D_MODEL // 2
A_GROUPS = 8
A_GROUP_DIM = A_WIDTH // A_GROUPS
A_BLOCK = 128
B_WIDTH = D_MODEL // 2
POOL_WINDOWS = (2, 4, 8, 16)
B_GROUPS = len(POOL_WINDOWS)
B_GROUP_DIM = B_WIDTH // B_GROUPS
POOL_PAD = max(POOL_WINDOWS) - 1
RET_HEADS = 8
RET_QK_DIM = D_MODEL // RET_HEADS
RET_VW = 2 * D_MODEL
RET_V_DIM = RET_VW // RET_HEADS
ROPE_BASE = 10000.0
D_FF = -(-8 * D_MODEL // (3 * 256)) * 256

kernel_name = 'streaming_gmlp_pool_retention_step'


def _rmsnorm(x, g):
    xf = x.astype(jnp.float32)
    y = xf * lax.rsqrt(jnp.mean(xf * xf, axis=-1, keepdims=True) + EPS)
    return (y * g.astype(jnp.float32)).astype(x.dtype)


def _layernorm(x, g, b):
    xf = x.astype(jnp.float32)
    mu = jnp.mean(xf, axis=-1, keepdims=True)
    xc = xf - mu
    y = xc * lax.rsqrt(jnp.mean(xc * xc, axis=-1, keepdims=True) + EPS)
    return (y * g.astype(jnp.float32) + b.astype(jnp.float32)).astype(x.dtype)


def _ada(c, w, b):
    m = jax.nn.silu(c) @ w + b
    return [t[:, None, :] for t in jnp.split(m, 6, axis=-1)]


def _modulate(h, shift, scale):
    return h * (1 + scale) + shift


def _spatial_gate(u, v, w_s, b_s):
    Bn, S, _ = v.shape
    blk = min(S, A_BLOCK)
    nb = S // blk
    idx = jnp.arange(blk)
    mask = (idx[None, :] // CHUNK) <= (idx[:, None] // CHUNK)
    ws = jnp.where(mask[None], w_s[:, :blk, :blk], 0.0).astype(v.dtype)
    vb = v.reshape(Bn, nb, blk, A_GROUPS, A_GROUP_DIM)
    mixed = jnp.einsum('gts,bnsgc->bntgc', ws, vb) + b_s[:, :blk].T[None, None, :, :, None]
    return u * mixed.reshape(Bn, S, A_WIDTH)


def _pool_mix(xb_ext, start, w_pool, pool_scale):
    L = xb_ext.shape[1] - POOL_PAD
    xf = xb_ext.astype(jnp.float32)
    cs = jnp.cumsum(xf, axis=1)
    cs = jnp.concatenate([jnp.zeros_like(cs[:, :1]), cs], axis=1)
    pos = start + jnp.arange(L)
    outs = []
    for g, w in enumerate(POOL_WINDOWS):
        sl = slice(g * B_GROUP_DIM, (g + 1) * B_GROUP_DIM)
        hi = cs[:, POOL_PAD + 1:POOL_PAD + 1 + L, sl]
        lo = cs[:, POOL_PAD + 1 - w:POOL_PAD + 1 - w + L, sl]
        cnt = jnp.minimum(w, pos + 1).astype(jnp.float32)[None, :, None]
        outs.append((hi - lo) / cnt - xf[:, POOL_PAD:, sl])
    pooled = jnp.stack(outs, axis=2)
    y = jnp.einsum('blgc,gcd->blgd', pooled, w_pool.astype(jnp.float32))
    y = y.reshape(xb_ext.shape[0], L, B_WIDTH) * pool_scale.astype(jnp.float32)
    return y.astype(xb_ext.dtype)


def _even_mixer(h, pool_hist, start, w_in, ln_g, ln_b, w_s, b_s, w_pool, pool_scale, w_out):
    proj = h @ w_in
    u, v, xb = jnp.split(proj, [A_WIDTH, 2 * A_WIDTH], axis=-1)
    u = jax.nn.gelu(u)
    v = _layernorm(jax.nn.gelu(v), ln_g, ln_b)
    ya = _spatial_gate(u, v, w_s, b_s)
    xb_ext = jnp.concatenate([pool_hist.astype(xb.dtype), xb], axis=1)
    yb = _pool_mix(xb_ext, start, w_pool, pool_scale)
    y = jnp.concatenate([ya, yb], axis=-1) @ w_out
    return y, xb_ext[:, -POOL_PAD:], v


def _rope(x, pos):
    half = RET_QK_DIM // 2
    freq = ROPE_BASE ** (-jnp.arange(half, dtype=jnp.float32) / half)
    ang = pos.astype(jnp.float32)[:, None] * freq[None, :]
    cos = jnp.cos(ang)[None, :, None, :]
    sin = jnp.sin(ang)[None, :, None, :]
    x1, x2 = x[..., :half], x[..., half:]
    return jnp.concatenate([x1 * cos - x2 * sin, x1 * sin + x2 * cos], axis=-1)


def _retention_block(s_prev, q, k, v, log_g):
    L = q.shape[1]
    n = jnp.arange(L, dtype=jnp.float32)
    diff = n[:, None] - n[None, :]
    decay = jnp.where(diff >= 0, jnp.exp(jnp.maximum(diff, 0.0)[None] * log_g[:, None, None]), 0.0)
    scores = jnp.einsum('blhd,bmhd->bhlm', q, k) * decay[None]
    intra = jnp.einsum('bhlm,bmhe->blhe', scores, v)
    q_dec = q * jnp.exp((n[:, None] + 1.0) * log_g[None, :])[None, :, :, None]
    cross = jnp.einsum('blhd,bhde->blhe', q_dec, s_prev)
    k_dec = k * jnp.exp((L - 1.0 - n)[:, None] * log_g[None, :])[None, :, :, None]
    s_new = s_prev * jnp.exp(L * log_g)[None, :, None, None] + jnp.einsum('blhd,blhe->bhde', k_dec, v)
    return s_new, intra + cross


def _retention(q, k, v, s0, log_g):
    Bn, S = q.shape[:2]
    blk = min(S, CHUNK)
    nb = S // blk

    def to_blocks(t):
        return jnp.moveaxis(t.reshape(Bn, nb, blk, *t.shape[2:]), 1, 0)

    def step(s, qkv):
        return _retention_block(s, qkv[0], qkv[1], qkv[2], log_g)

    s_fin, o = lax.scan(step, s0, (to_blocks(q), to_blocks(k), to_blocks(v)))
    o = jnp.moveaxis(o, 0, 1).reshape(Bn, S, RET_HEADS, RET_V_DIM)
    return o, s_fin


def _odd_mixer(h, s0, start, w_in, w_out):
    Bn, S, _ = h.shape
    proj = h @ w_in
    q, k, v, g = jnp.split(proj, [D_MODEL, 2 * D_MODEL, 2 * D_MODEL + RET_VW], axis=-1)
    q = q.astype(jnp.float32).reshape(Bn, S, RET_HEADS, RET_QK_DIM)
    k = k.astype(jnp.float32).reshape(Bn, S, RET_HEADS, RET_QK_DIM) * (RET_QK_DIM ** -0.5)
    v = v.astype(jnp.float32).reshape(Bn, S, RET_HEADS, RET_V_DIM)
    pos = start + jnp.arange(S)
    q = _rope(q, pos)
    k = _rope(k, pos)
    log_g = jnp.log1p(-jnp.exp2(-5.0 - jnp.arange(RET_HEADS, dtype=jnp.float32)))
    o, s_fin = _retention(q, k, v, s0.astype(jnp.float32), log_g)
    o = o * lax.rsqrt(jnp.mean(o * o, axis=-1, keepdims=True) + EPS)
    o = o.reshape(Bn, S, RET_VW).astype(h.dtype)
    y = (jax.nn.silu(g) * o) @ w_out
    return y, s_fin.astype(h.dtype)


def _swiglu(h, w_gu, w_down):
    gate, up = jnp.split(h @ w_gu, 2, axis=-1)
    return (jax.nn.silu(gate) * up) @ w_down


def setup_inputs(seed: int = 0) -> dict:
    key = jax.random.key(seed)
    ks = jax.random.split(key, 24)
    nrm = jax.random.normal
    f32 = jnp.float32
    D = D_MODEL
    return {
        'x_prompt': nrm(ks[0], (BATCH, SEQ, D), f32),
        'x_sample': nrm(ks[1], (DEC_BATCH, DEC_SEQ, D), f32),
        'c_prompt': nrm(ks[2], (BATCH, D), f32),
        'c_sample': nrm(ks[3], (DEC_BATCH, D), f32),
        'state_b_pool': nrm(ks[4], (N_EVEN, DEC_BATCH, POOL_PAD, B_WIDTH), f32),
        'state_c_ret': 0.5 * nrm(ks[5], (N_ODD, DEC_BATCH, RET_HEADS, RET_QK_DIM, RET_V_DIM), f32),
        'w_ada': 0.5 * D ** -0.5 * nrm(ks[6], (DEPTH, D, 6 * D), f32),
        'b_ada': 0.01 * nrm(ks[7], (DEPTH, 6 * D), f32),
        'norm_mix': 1.0 + 0.1 * nrm(ks[8], (DEPTH, D), f32),
        'norm_ffn': 1.0 + 0.1 * nrm(ks[9], (DEPTH, D), f32),
        'norm_final': 1.0 + 0.1 * nrm(ks[10], (D,), f32),
        'w_in_ab': D ** -0.5 * nrm(ks[11], (N_EVEN, D, 3 * A_WIDTH), f32),
        'ln_v_g': 1.0 + 0.1 * nrm(ks[12], (N_EVEN, A_WIDTH), f32),
        'ln_v_b': 0.01 * nrm(ks[13], (N_EVEN, A_WIDTH), f32),
        'w_s': 0.5 * A_BLOCK ** -0.5 * nrm(ks[14], (N_EVEN, A_GROUPS, A_BLOCK, A_BLOCK), f32),
        'b_s': 1.0 + 0.1 * nrm(ks[15], (N_EVEN, A_GROUPS, A_BLOCK), f32),
        'w_pool': B_GROUP_DIM ** -0.5 * nrm(ks[16], (N_EVEN, B_GROUPS, B_GROUP_DIM, B_GROUP_DIM), f32),
        'pool_scale': 1.0 + 0.1 * nrm(ks[17], (N_EVEN, B_WIDTH), f32),
        'w_out_ab': (A_WIDTH + B_WIDTH) ** -0.5 * nrm(ks[18], (N_EVEN, A_WIDTH + B_WIDTH, D), f32),
        'w_in_c': D ** -0.5 * nrm(ks[19], (N_ODD, D, 2 * D + 2 * RET_VW), f32),
        'w_out_c': RET_VW ** -0.5 * nrm(ks[20], (N_ODD, RET_VW, D), f32),
        'w_ffn_gu': D ** -0.5 * nrm(ks[21], (DEPTH, D, 2 * D_FF), f32),
        'w_ffn_down': D_FF ** -0.5 * nrm(ks[22], (DEPTH, D_FF, D), f32),
    }


def reference(x_prompt, x_sample, c_prompt, c_sample, state_b_pool, state_c_ret,
              w_ada, b_ada, norm_mix, norm_ffn, norm_final,
              w_in_ab, ln_v_g, ln_v_b, w_s, b_s, w_pool, pool_scale, w_out_ab,
              w_in_c, w_out_c, w_ffn_gu, w_ffn_down):
    bp = x_prompt.shape[0]
    zero_pool = jnp.zeros((bp, POOL_PAD, B_WIDTH), x_prompt.dtype)
    zero_ret = jnp.zeros((bp, RET_HEADS, RET_QK_DIM, RET_V_DIM), jnp.float32)
    xp, xs = x_prompt, x_sample
    pool_p, pool_s, v_s, ret_p, ret_s = [], [], [], [], []
    for layer in range(DEPTH):
        mp = _ada(c_prompt, w_ada[layer], b_ada[layer])
        ms = _ada(c_sample, w_ada[layer], b_ada[layer])
        hp = _modulate(_rmsnorm(xp, norm_mix[layer]), mp[0], mp[1])
        hs = _modulate(_rmsnorm(xs, norm_mix[layer]), ms[0], ms[1])
        if layer % 2 == 0:
            e = layer // 2
            prm = (w_in_ab[e], ln_v_g[e], ln_v_b[e], w_s[e], b_s[e], w_pool[e], pool_scale[e], w_out_ab[e])
            op, new_pp, _ = _even_mixer(hp, zero_pool, 0, *prm)
            osm, new_ps, new_vs = _even_mixer(hs, state_b_pool[e], PAST_LEN, *prm)
            pool_p.append(new_pp)
            pool_s.append(new_ps)
            v_s.append(new_vs)
        else:
            o = layer // 2
            op, new_rp = _odd_mixer(hp, zero_ret, 0, w_in_c[o], w_out_c[o])
            osm, new_rs = _odd_mixer(hs, state_c_ret[o], PAST_LEN, w_in_c[o], w_out_c[o])
            ret_p.append(new_rp)
            ret_s.append(new_rs)
        xp = xp + mp[2] * op
        xs = xs + ms[2] * osm
        hp = _modulate(_rmsnorm(xp, norm_ffn[layer]), mp[3], mp[4])
        hs = _modulate(_rmsnorm(xs, norm_ffn[layer]), ms[3], ms[4])
        xp = xp + mp[5] * _swiglu(hp, w_ffn_gu[layer], w_ffn_down[layer])
        xs = xs + ms[5] * _swiglu(hs, w_ffn_gu[layer], w_ffn_down[layer])
    y_prompt = _rmsnorm(xp, norm_final)
    y_sample = _rmsnorm(xs, norm_final)
    return (y_prompt, y_sample, jnp.stack(pool_p), jnp.stack(pool_s), jnp.stack(v_s), jnp.stack(ret_p), jnp.stack(ret_s))
```

```python
import numpy as np
import concourse.bass as bass
import concourse.mybir as mybir
from concourse.bass_utils import run_bass_kernel_spmd

F32, BF16 = mybir.dt.float32, mybir.dt.bfloat16
AF = mybir.ActivationFunctionType
ALU = mybir.AluOpType
AX = mybir.AxisListType

NCORES = 8
D = 2048
T = 2112
TP = 2048
TILES = [(0, 512), (512, 512), (1024, 512), (1536, 512), (2048, 64)]
DFF = 5632
EPS = 1e-6
NSEM_DMA = 6
ARENA_BYTES = 198 * 1024


class Op:
    __slots__ = ("id", "eng", "fn", "deps", "dma", "sig", "idx", "dsem", "dval")


class Buf:
    def __init__(self, name, start, nbytes):
        self.name, self.start, self.nbytes = name, start, nbytes
        self.inherited = set()
        self.keys = set()


class Prog:
    ENGS = ("pe", "act", "dve", "pool", "sp")

    def __init__(self):
        self.ops = []
        self.by_eng = {e: [] for e in self.ENGS}
        self.lastw = {}
        self.readers = {}
        self.dma_n = {e: 0 for e in self.ENGS}
        self.dma_slot_last = {}
        self.free_list = [(0, ARENA_BYTES)]
        self.freed = []
        self.peak = 0
        self.live = {}

    def alloc(self, name, nbytes):
        nbytes = (nbytes + 63) // 64 * 64
        for i, (s, e) in enumerate(self.free_list):
            if e - s >= nbytes:
                self.free_list[i] = (s + nbytes, e)
                b = Buf(name, s, nbytes)
                for (fs, fe, opset) in self.freed:
                    if fs < s + nbytes and s < fe:
                        b.inherited |= opset
                self.live[name] = b
                self.peak = max(self.peak, max(x.start + x.nbytes for x in self.live.values()))
                return b
        raise RuntimeError(f"arena full allocating {name} {nbytes}: {self.free_list} live={[(k, v.nbytes) for k, v in self.live.items()]}")

    def free(self, b):
        opset = set(b.inherited)
        for k in b.keys:
            if self.lastw.get(k) is not None:
                opset.add(self.lastw[k])
            opset |= set(self.readers.get(k, ()))
            self.lastw.pop(k, None)
            self.readers.pop(k, None)
        opset = self._compress(opset)
        self.freed = [(fs, fe, o) for (fs, fe, o) in self.freed if not (fs >= b.start and fe <= b.start + b.nbytes)]
        self.freed.append((b.start, b.start + b.nbytes, opset))
        del self.live[b.name]
        fl = self.free_list + [(b.start, b.start + b.nbytes)]
        fl.sort()
        merged = []
        for s, e in fl:
            if s == e:
                continue
            if merged and merged[-1][1] == s:
                merged[-1] = (merged[-1][0], e)
            else:
                merged.append((s, e))
        self.free_list = merged

    def _compress(self, opset):
        best = {}
        for i in opset:
            o = self.ops[i]
            k = (o.eng, o.dsem) if o.dma else (o.eng, None)
            if k not in best or best[k] < i:
                best[k] = i
        return set(best.values())

    def emit(self, eng, fn, reads=(), writes=(), dma=False, cc=False):
        o = Op()
        o.id = len(self.ops)
        o.eng, o.fn, o.dma = eng, fn, dma
        o.sig, o.idx, o.dsem, o.dval = False, 0, None, 0
        deps = set()
        for k in list(reads) + list(writes):
            if isinstance(k, tuple) and isinstance(k[0], Buf):
                b = k[0]
                if k not in b.keys:
                    b.keys.add(k)
                    deps |= b.inherited
        for k in reads:
            w = self.lastw.get(k)
            if w is not None:
                deps.add(w)
        for k in writes:
            w = self.lastw.get(k)
            if w is not None:
                deps.add(w)
            deps |= set(self.readers.get(k, ()))
        if dma:
            n = self.dma_n[eng]
            self.dma_n[eng] = n + 1
            slot = n % NSEM_DMA
            o.dsem = (eng, "cc") if cc else (eng, slot)
            inc = 1 if cc else 16
            prev = self.dma_slot_last.get(o.dsem)
            if prev is not None:
                if not cc:
                    deps.add(prev)
                o.dval = self.ops[prev].dval + inc
            else:
                o.dval = inc
            self.dma_slot_last[o.dsem] = o.id
            o.sig = True
        o.deps = deps
        self.ops.append(o)
        self.by_eng[eng].append(o)
        for k in reads:
            lst = self.readers.setdefault(k, [])
            if not dma:
                lst[:] = [r for r in lst if self.ops[r].dma or self.ops[r].eng != eng]
            lst.append(o.id)
        for k in writes:
            self.lastw[k] = o.id
            self.readers[k] = []
        return o

    def finalize(self, nc):
        for o in self.ops:
            for d in o.deps:
                p = self.ops[d]
                if not p.dma and not (p.eng == "pe" and o.eng == "pe" and not o.dma):
                    p.sig = True
        for e in self.ENGS:
            n = 0
            for o in self.by_eng[e]:
                if not o.dma and o.sig:
                    n += 1
                    o.idx = n
        import contextlib
        with contextlib.ExitStack() as st:
            esem = {e: st.enter_context(nc.semaphore("s_" + e)) for e in self.ENGS}
            dsem = {}
            for e in self.ENGS:
                if self.dma_n[e]:
                    for s in range(NSEM_DMA):
                        dsem[(e, s)] = st.enter_context(nc.semaphore(f"d_{e}{s}"))
                    dsem[(e, "cc")] = st.enter_context(nc.semaphore(f"c_{e}"))
            fin = st.enter_context(nc.semaphore("fin"))
            block = st.enter_context(nc.Block())
            prog = self

            def run(eng_name, e):
                waited = {}
                for o in prog.by_eng[eng_name]:
                    need = {}
                    for d in o.deps:
                        p = prog.ops[d]
                        if p.dma:
                            sem, val = dsem[p.dsem], p.dval
                        else:
                            if p.eng == "pe" and eng_name == "pe" and not o.dma:
                                continue
                            sem, val = esem[p.eng], p.idx
                        key = id(sem)
                        if waited.get(key, 0) >= val:
                            continue
                        if key not in need or need[key][1] < val:
                            need[key] = (sem, val)
                    for key, (sem, val) in need.items():
                        e.wait_ge(sem, val)
                        waited[key] = val
                    ins = o.fn(e)
                    if o.dma and o.dsem[1] == "cc":
                        ins.then_inc(dsem[o.dsem])
                    elif o.dma:
                        ins.then_inc(dsem[o.dsem], 16)
                    elif o.sig:
                        ins.then_inc(esem[eng_name], 1)
                last_sig = None
                return

            finals_d = {}
            for o in prog.ops:
                if o.dma:
                    finals_d[o.dsem] = max(finals_d.get(o.dsem, 0), o.dval)
            finals_e = {e: max([o.idx for o in prog.by_eng[e] if not o.dma] + [0]) for e in self.ENGS}

            @block.tensor
            def _(e):
                run("pe", e)

            @block.scalar
            def _(e):
                run("act", e)

            @block.vector
            def _(e):
                run("dve", e)

            @block.gpsimd
            def _(e):
                run("pool", e)

            @block.sync
            def _(e):
                run("sp", e)
                for k, v in finals_d.items():
                    e.wait_ge(dsem[k], v)
                for en, v in finals_e.items():
                    if v:
                        e.wait_ge(esem[en], v)


def build_program(debug=False):
    nc = bass.Bass("TRN2", target_bir_lowering=False)
    P = Prog()

    def din(name, shape, dt=F32):
        return nc.dram_tensor(name, list(shape), dt, kind="ExternalInput").ap()

    def dout(name, shape, dt=F32):
        return nc.dram_tensor(name, list(shape), dt, kind="ExternalOutput").ap()

    def dscr(name, shape, dt):
        return nc.dram_tensor(name, list(shape), dt).ap()

    xT_in = din("xT", [D, T])
    xhT_in = din("xhT", [D, 16])
    c3_in = din("c3", [128, 16, 3])
    cflag_in = din("cflag", [128, 8])
    icnt_in = din("icnt", [128, 4, 16])
    phist_in = din("phist", [128, 8, 2, 16])
    w_ada = din("w_ada_sh", [2, D, 1536])
    bada_rows = din("bada_rows", [18, 2, 1536])
    c18_in = din("c18T", [128, 16, 18])
    selT_in = din("selT", [18, 4])
    normw = din("normw", [128, 5, 16])
    w_in_ab = din("w_in_ab", [D, 3072])
    lng_in = din("lng", [128, 1024])
    lnb_in = din("lnb", [128, 1024])
    wsT_in = din("wsT", [128, 8, 128])
    wsTs_in = din("wsTs", [64, 8, 64])
    bsbc_in = din("bsbc", [128, 8, 128])
    bsbcs_in = din("bsbcs", [128, 8, 64])
    w_pool = din("w_pool", [4, 256, 256])
    pscale_in = din("pscaleT", [128, 8])
    w_out_ab = din("w_out_ab", [D, D])
    w_gu = din("w_ffn_gu", [2, D, 2 * DFF])
    w_down = din("w_ffn_down", [2, DFF, D])
    ident_in = din("ident", [128, 128])
    w_in_c = din("w_in_c", [D, 12288])
    w_out_c = din("w_out_c", [4096, D])
    sret_in = din("sret", [2, 8, 256, 512])
    cos_in = din("cosT", [128, T])
    sin_in = din("sinT", [128, T])
    dtab_in = din("dtab", [128, 32])
    decT_in = din("decT", [128, 8, 128])
    coef_in = din("coef", [128, 64])
    dtab2_in = din("dtab2", [128, 128])

    yT_out = dout("yT", [D, T])
    pool_out = dout("pool_o", [128, 8, 3, 16])
    vs_out = dout("vs_o", [64, 1024])
    retp_out = dout("retp_o", [8, 256, 512])
    rets_out = dout("rets_o", [2, 8, 256, 512])

    xs = dscr("xs", [D, T], F32)
    ybT_d = dscr("ybT", [1024, T], BF16)
    aT_d = dscr("aT", [DFF, T], BF16)
    qT_d = dscr("qT", [D, T], BF16)
    kT_d = dscr("kT", [D, T], BF16)
    kd_d = dscr("kd", [17 * 128, D], BF16)
    kd2_d = dscr("kd2", [16 * 128, D], BF16)
    v_d = dscr("vtok", [17 * 128, 4096], BF16)
    sg_d = dscr("sgtok", [17 * 128, 4096], BF16)
    ogT_d = dscr("ogT", [4096, T], BF16)
    sloc_d = [nc.dram_tensor(f"sloc{h}", [256, 512], F32) for h in range(8)]
    mloc_d = nc.dram_tensor("mloc", [18, 3072], F32)
    mall_d = nc.dram_tensor("mall", [8 * 18, 3072], F32)
    sall_d = [nc.dram_tensor(f"sall{h}", [8 * 256, 512], F32) for h in range(8)]

    dbg = {}
    if debug:
        dbg["x_l0"] = dout("dbg_x_l0", [D, T])

    import contextlib
    with contextlib.ExitStack() as stack:
        arena = stack.enter_context(nc.sbuf_tensor("arena", [128, ARENA_BYTES // 4], F32))
        cst = stack.enter_context(nc.sbuf_tensor("cst", [128, 2048], F32))
        psum = stack.enter_context(nc.psum_tensor("psum", [128, 8, 512], F32))

        def view(buf, dt, shape):
            n4 = buf.nbytes // 4
            ap = arena[:, buf.start // 4: buf.start // 4 + n4]
            if dt != F32:
                ap = ap.bitcast(dt)
            esz = 4 if dt == F32 else 2
            tot = 1
            for s in shape:
                tot *= s
            assert tot * esz <= buf.nbytes, (buf.name, shape, buf.nbytes)
            ap = ap[:, 0:tot]
            if len(shape) == 1:
                return ap
            names = " ".join(f"a{i}" for i in range(len(shape)))
            kw = {f"a{i}": s for i, s in enumerate(shape[:-1])}
            return ap.rearrange(f"p ({names}) -> p {names}", **kw)

        _cpos = [0]

        def cst_alloc(n):
            s = _cpos[0]
            _cpos[0] += n
            assert _cpos[0] <= 2048
            return cst[:, s:s + n]

        ident_f = cst_alloc(128)
        ones_f = cst_alloc(128)
        normw_s = cst_alloc(80).rearrange("p (a c) -> p a c", a=5)
        cflag_s = cst_alloc(8)
        mod_s = [cst_alloc(288).rearrange("p (v c r) -> p v c r", v=6, c=16) for _ in range(2)]
        Amix = [cst_alloc(48).rearrange("p (c r) -> p c r", c=16) for _ in range(2)]
        Affn = [cst_alloc(48).rearrange("p (c r) -> p c r", c=16) for _ in range(2)]
        bada_s = cst_alloc(192).rearrange("p (l c) -> p l c", l=2)
        pscale_s = cst_alloc(8)
        icnt_s = cst_alloc(64).rearrange("p (g t) -> p g t", g=4)
        c3_s = cst_alloc(48).rearrange("p (c r) -> p c r", c=16)
        stat_s = cst_alloc(64)
        identb = cst_alloc(64).bitcast(BF16)
        sc_b = cst_alloc(24).bitcast(BF16).rearrange("p (c r) -> p c r", c=16)

        CK = ("cst",)

        def PS(b):
            return ("ps", b)

        ps_rr = [0]

        def next_bank():
            b = ps_rr[0] % 8
            ps_rr[0] += 1
            return b

        def dma(q, out, in_, reads, writes):
            return P.emit(q, lambda e, o=out, i=in_: e.dma_start(out=o, in_=i), reads, writes, dma=True)

        def act(out, in_, func, reads, writes, bias=None, scale=None, accum_out=None):
            kw = {}
            if bias is not None:
                kw["bias"] = bias
            if scale is not None:
                kw["scale"] = scale
            if accum_out is not None:
                kw["accum_out"] = accum_out
            return P.emit("act", lambda e, o=out, i=in_, f=func, kw=kw: e.activation(out=o, in_=i, func=f, **kw), reads, writes)

        def tt(eng, out, in0, in1, op, reads, writes):
            return P.emit(eng, lambda e, o=out, a=in0, b=in1, op=op: e.tensor_tensor(out=o, in0=a, in1=b, op=op), reads, writes)

        def ts(eng, out, in0, s1, s2, op0, op1, reads, writes):
            if op1 is None:
                return P.emit(eng, lambda e, o=out, a=in0, s1=s1, op0=op0: e.tensor_scalar(out=o, in0=a, scalar1=s1, scalar2=None, op0=op0), reads, writes)
            return P.emit(eng, lambda e, o=out, a=in0, s1=s1, s2=s2, op0=op0, op1=op1: e.tensor_scalar(out=o, in0=a, scalar1=s1, scalar2=s2, op0=op0, op1=op1), reads, writes)

        def stt(eng, out, in0, scalar, in1, op0, op1, reads, writes):
            return P.emit(eng, lambda e, o=out, a=in0, s=scalar, b=in1, op0=op0, op1=op1: e.scalar_tensor_tensor(out=o, in0=a, scalar=s, in1=b, op0=op0, op1=op1), reads, writes)

        def copy(eng, out, in_, reads, writes):
            if eng == "act":
                return P.emit("act", lambda e, o=out, i=in_: e.copy(out=o, in_=i), reads, writes)
            return P.emit(eng, lambda e, o=out, i=in_: e.tensor_copy(out=o, in_=i), reads, writes)

        def mm(out, lhsT, rhs, start, stop, reads, writes):
            return P.emit("pe", lambda e, o=out, l=lhsT, r=rhs, s=start, t=stop: e.matmul(o, lhsT=l, rhs=r, start=s, stop=t), reads, writes)

        def memset(eng, ap, val, writes):
            return P.emit(eng, lambda e, a=ap, v=val: e.memset(a, v), (), writes)

        dma("sp", ident_f, ident_in, (), [CK])
        dma("sp", normw_s, normw, (), [CK])
        dma("sp", cflag_s, cflag_in, (), [CK])
        dma("sp", pscale_s, pscale_in, (), [CK])
        dma("sp", icnt_s, icnt_in, (), [CK])
        dma("sp", c3_s, c3_in, (), [CK])
        memset("dve", ones_f, 1.0, [CK])
        copy("dve", identb, ident_f, [CK], [("identb",)])

        class _W:
            pass
        W_ = _W()

        def ws_open():
            W_.buf = P.alloc("wslots", 3 * 16384)
            W_.slots = [view(W_.buf, BF16, [3, 16, 512])[:, i] for i in range(3)]

        def ws_close():
            P.free(W_.buf)
        ws_open()
        wrr = [0]

        def wslot_next():
            i = wrr[0] % 3
            wrr[0] += 1
            return i

        def load_w16(src3d, ncols):
            i = wslot_next()
            dma("pool", W_.slots[i][:, :, 0:ncols], src3d, (), [(W_.buf, i)])
            return i

        def ada_all():
            ab = P.alloc("adatmp", 8 * 3072 * 4)
            mall = view(ab, F32, [8, 3072])
            cb = P.alloc("adac", 16 * 18 * 4 + 16 * 18 * 2 + 2 * 1536 * 4 + 2 * 1536 * 4 + 64)
            b4 = cb.start // 4
            c18 = arena[:, b4:b4 + 288].rearrange("p (c q) -> p c q", c=16)
            sc18 = arena[:, b4 + 288:b4 + 432].bitcast(BF16).rearrange("p (c q) -> p c q", c=16)
            brow = arena[:, b4 + 432:b4 + 432 + 3072].rearrange("p (l n) -> p l n", l=2)
            mp = arena[:, b4 + 3504:b4 + 3504 + 3072].rearrange("p (l n) -> p l n", l=2)
            selT = arena[:, b4 + 6576:b4 + 6580]
            AK_ = (cb, 0)
            dma("sp", c18, c18_in, (), [AK_])
            dma("sp", brow[0:18], bada_rows, (), [AK_])
            dma("sp", selT[0:18], selT_in, (), [AK_])
            act(sc18, c18, AF.Silu, [AK_], [(cb, 1)])
            for l in range(2):
                wv = w_ada[l].rearrange("(c p) n -> p c n", p=128)
                for blk in range(3):
                    slot = load_w16(wv[:, :, blk * 512:(blk + 1) * 512], 512)
                    bank = next_bank()
                    for k in range(16):
                        mm(psum[0:18, bank, :], sc18[:, k, :], W_.slots[slot][:, k, :], k == 0, k == 15, [(W_.buf, slot), (cb, 1)], [PS(bank)])
                    tt("dve", mp[0:18, l, blk * 512:(blk + 1) * 512], psum[0:18, bank, :], brow[0:18, l, blk * 512:(blk + 1) * 512], ALU.add,
                       [PS(bank), AK_], [(cb, 2)])
            dma("sp", mloc_d.ap(), mp[0:18].rearrange("p l n -> p (l n)"), [(cb, 2)], [("mloc",)])
            P.emit("pool", lambda e: e.collective_compute("AllGather", ALU.bypass, replica_groups=[list(range(8))],
                                                         ins=[mloc_d.ap().opt()], outs=[mall_d.ap().opt()]),
                   [("mloc",)], [("mall",)], dma=True, cc=True)
            dma("sp", mall[0:18], mall_d.ap().rearrange("(r q) c -> q r c", q=18), [("mall",)], [(ab, 0)])
            for l in range(2):
                bank = next_bank()
                pst = psum[:, bank, 0:288].rearrange("p (j r) -> p j r", r=3)
                for r in range(8):
                    for n in range(12):
                        j = r * 12 + n
                        mm(pst[:, j, :], mall[0:18, r, l * 1536 + n * 128:l * 1536 + (n + 1) * 128], selT[0:18, 0:3], True, True,
                           [(ab, 0), AK_], [PS(bank)])
                mflat = mod_s[l].rearrange("p v c r -> p (v c) r")
                copy("dve", mflat, pst, [PS(bank)], [("mod", l)])
                for (Adst, vi, nw) in ((Amix[l], 1, l), (Affn[l], 4, 2 + l)):
                    stt("dve", Adst, mod_s[l][:, vi], 1.0, normw_s[:, nw, :].unsqueeze(2).to_broadcast([128, 16, 3]),
                        ALU.add, ALU.mult, [("mod", l), CK], [("modA", l)])
            P.free(ab)
            P.free(cb)

        def colr(c0, w):
            if c0 < TP:
                return [(0, w, 0)]
            return [(0, 32, 1), (32, 32, 2)]

        def norm_phase(xsrc, hT, hbuf, A, Bmod, l, from_xs=True):
            ws_close()
            xv = xsrc.rearrange("(c p) t -> p c t", p=128)
            NX, NQ = 4, 3
            xb_ = [P.alloc(f"nx{i}", 16 * 256 * 4) for i in range(NX)]
            sqb_ = [P.alloc(f"nsq{i}", 16 * 256 * 4) for i in range(NQ)]
            rsb_ = [P.alloc(f"nrs{i}", 3 * 256 * 4) for i in range(NQ)]
            tiles = [(i * 256, 256) for i in range(8)] + [(TP, 64)]
            banks = {}

            def bufs(si):
                c0, w = tiles[si]
                xb, sqb, rsb = xb_[si % NX], sqb_[si % NQ], rsb_[si % NQ]
                return (c0, w, xb, sqb, rsb, view(rsb, F32, [3, 256]), view(xb, F32, [16, 256])[:, :, 0:w], view(sqb, F32, [16, 256])[:, :, 0:w])

            def stage0(si):
                c0, w, xb, sqb, rsb, rs, xt, sq = bufs(si)
                dma("sp", xt, xv[:, :, c0:c0 + w], [("xs", n, (c0 // 512) * 512 if c0 < TP else TP) for n in range(16)] if from_xs else (), [(xb, 0)])

            def stage1(si):
                c0, w, xb, sqb, rsb, rs, xt, sq = bufs(si)
                act(sq, xt, AF.Square, [(xb, 0)], [(sqb, 0)])
                P.emit("dve", lambda e, o=rs[:, 0, 0:w], i=sq.rearrange("p c t -> p t c"): e.tensor_reduce(out=o, in_=i, axis=AX.X, op=ALU.add),
                       [(sqb, 0)], [(rsb, 0)])
                bank = next_bank()
                banks[si] = bank
                mm(psum[:, bank, 0:w], ones_f, rs[:, 0, 0:w], True, True, [CK, (rsb, 0)], [PS(bank)])

            def stage2(si):
                c0, w, xb, sqb, rsb, rs, xt, sq = bufs(si)
                bank = banks[si]
                act(rs[:, 1, 0:w], psum[:, bank, 0:w], AF.Sqrt, [PS(bank)], [(rsb, 1)], bias=eps_ap, scale=1.0 / D)
                P.emit("dve", lambda e, o=rs[:, 2, 0:w], i=rs[:, 1, 0:w]: e.reciprocal(out=o, in_=i), [(rsb, 1)], [(rsb, 2)])
                tt("pool", sq, xt, rs[:, 2, 0:w].unsqueeze(1).to_broadcast([128, 16, w]), ALU.mult, [(xb, 0), (rsb, 2)], [(sqb, 0)])

            def stage3(si):
                c0, w, xb, sqb, rsb, rs, xt, sq = bufs(si)
                ti = min(c0 // 512, 4)
                for c in range(16):
                    for (o0, ww, r) in colr(c0, w):
                        if c % 4 == 3:
                            ts("dve", hT[:, c, c0 + o0:c0 + o0 + ww], sq[:, c, o0:o0 + ww], A[:, c, r:r + 1], Bmod[:, c, r:r + 1], ALU.mult, ALU.add,
                               [(sqb, 0), ("modA", l), ("mod", l)], [(hbuf, ti)])
                        else:
                            act(hT[:, c, c0 + o0:c0 + o0 + ww], sq[:, c, o0:o0 + ww], AF.Identity, [(sqb, 0), ("modA", l), ("mod", l)],
                                [(hbuf, ti)], bias=Bmod[:, c, r:r + 1], scale=A[:, c, r:r + 1])
            n = len(tiles)
            for si in range(min(2, n)):
                stage0(si)
            for si in range(n + 2):
                if si + 2 < n:
                    stage0(si + 2)
                if si < n:
                    stage1(si)
                if 1 <= si <= n:
                    stage2(si - 1)
                if si >= 2:
                    stage3(si - 2)
            for b in xb_ + sqb_ + rsb_:
                P.free(b)
            ws_open()

        eps_ap = cst_alloc(1)
        memset("dve", eps_ap, EPS, [CK])

        def resid_update(xsrc, xdst, n, c0, w, bank, gate, l, xrb, slot):
            xt = view(xrb, F32, [4, 512])[:, slot, 0:w]
            dma("sp", xt, xsrc[n * 128:(n + 1) * 128, c0:c0 + w], [("xs", n, c0)], [(xrb, slot)])
            for (o0, ww, r) in colr(c0, w):
                stt("dve", xt[:, o0:o0 + ww], psum[:, bank, o0:o0 + ww], gate[:, n, r:r + 1], xt[:, o0:o0 + ww], ALU.mult, ALU.add,
                    [PS(bank), (xrb, slot), ("mod", l)], [(xrb, slot)])
            dma("sp", xdst[n * 128:(n + 1) * 128, c0:c0 + w], xt, [(xrb, slot)], [("xs", n, c0)])

        def layer0_mixer():
            l = 0
            hbuf = P.alloc("hT", 16 * T * 2)
            hT = view(hbuf, BF16, [16, T])
            norm_phase(xT_in, hT, hbuf, Amix[0], mod_s[0][:, 0], 0, from_xs=False)
            HK = [(hbuf, i) for i in range(5)]
            hhb = P.alloc("hh", 16 * 16 * 2 + 16 * 16 * 4 * 2 + 3 * 16 * 4)
            hh = view(hhb, BF16, [16, 16])
            hx = arena[:, (hhb.start + 512) // 4:(hhb.start + 512) // 4 + 256].rearrange("p (c t) -> p c t", c=16)
            hq = arena[:, (hhb.start + 512 + 1024) // 4:(hhb.start + 512 + 1024) // 4 + 256].rearrange("p (c t) -> p c t", c=16)
            hr = arena[:, (hhb.start + 512 + 2048) // 4:(hhb.start + 512 + 2048) // 4 + 48].rearrange("p (c t) -> p c t", c=3)
            dma("sp", hx, xhT_in.rearrange("(c p) t -> p c t", p=128), (), [(hhb, 0)])
            act(hq, hx, AF.Square, [(hhb, 0)], [(hhb, 1)])
            P.emit("dve", lambda e, o=hr[:, 0, :], i=hq.rearrange("p c t -> p t c"): e.tensor_reduce(out=o, in_=i, axis=AX.X, op=ALU.add), [(hhb, 1)], [(hhb, 2)])
            bank = next_bank()
            mm(psum[:, bank, 0:16], ones_f, hr[:, 0, :], True, True, [CK, (hhb, 2)], [PS(bank)])
            act(hr[:, 1, :], psum[:, bank, 0:16], AF.Sqrt, [PS(bank)], [(hhb, 3)], bias=eps_ap, scale=1.0 / D)
            P.emit("dve", lambda e, o=hr[:, 2, :], i=hr[:, 1, :]: e.reciprocal(out=o, in_=i), [(hhb, 3)], [(hhb, 4)])
            tt("dve", hq, hx, hr[:, 2, :].unsqueeze(1).to_broadcast([128, 16, 16]), ALU.mult, [(hhb, 0), (hhb, 4)], [(hhb, 1)])
            for c in range(16):
                act(hh[:, c, :], hq[:, c, :], AF.Identity, [(hhb, 1), ("modA", 0), ("mod", 0)], [(hhb, 5)],
                    bias=mod_s[0][:, 0, c, 0:1], scale=Amix[0][:, c, 0:1])

            wv = w_in_ab.rearrange("(c p) n -> p c n", p=128)

            SEG = [(0, 16 + TP), (16 + TP, 48), (16 + TP + 48, 48)]
            XL = 16 + TP + 96
            xbb = P.alloc("xb", 2 * XL * 4)
            p1b = P.alloc("pp1", 2 * XL * 4)
            p2b = P.alloc("pp2", 2 * XL * 4)
            pob = P.alloc("pooled", 2 * T * 2)
            ybs = P.alloc("ybstage", 2 * T * 2)
            wpb = P.alloc("wpool", 4 * 2 * 256 * 2)
            X = view(xbb, F32, [2, XL])
            P1 = view(p1b, F32, [2, XL])
            P2 = view(p2b, F32, [2, XL])
            pooled = view(pob, BF16, [2, T])
            ybst = view(ybs, BF16, [2, T])
            wp = view(wpb, BF16, [4, 2, 256])
            dma("pool", wp, w_pool.rearrange("g (c p) d -> p g c d", p=128), (), [(wpb, 0)])
            pool_ov = pool_out
            for g in range(4):
                win = 2 ** (g + 1)
                slot = load_w16(wv[:, :, 2048 + g * 256:2048 + (g + 1) * 256], 256)
                memset("pool", X[:, :, 0:1], 0.0, [(xbb, "h")])
                dma("sp", X[:, :, SEG[1][0]:SEG[1][0] + 16], phist_in[:, 2 * g:2 * g + 2, 0, :], (), [(xbb, "h1")])
                dma("sp", X[:, :, SEG[2][0]:SEG[2][0] + 16], phist_in[:, 2 * g:2 * g + 2, 1, :], (), [(xbb, "h2")])
                for cc in range(2):
                    bank = next_bank()
                    for k in range(16):
                        mm(psum[:, bank, 0:16], W_.slots[slot][:, k, cc * 128:(cc + 1) * 128], hh[:, k, :], k == 0, k == 15,
                           [(W_.buf, slot), (hhb, 5)], [PS(bank)])
                    ts("dve", X[:, cc, 0:16], psum[:, bank, 0:16], cflag_s[:, 0:1], None, ALU.mult, None, [PS(bank), CK], [(xbb, "h")])
                    for ti, (c0, w) in enumerate(TILES):
                        bank = next_bank()
                        for k in range(16):
                            mm(psum[:, bank, 0:w], W_.slots[slot][:, k, cc * 128:(cc + 1) * 128], hT[:, k, c0:c0 + w], k == 0, k == 15,
                               [(W_.buf, slot), HK[ti]], [PS(bank)])
                        if c0 < TP:
                            copy("act", X[:, cc, 16 + c0:16 + c0 + w], psum[:, bank, 0:w], [PS(bank)], [(xbb, ti)])
                        else:
                            copy("act", X[:, cc, SEG[1][0] + 16:SEG[1][0] + 48], psum[:, bank, 0:32], [PS(bank)], [(xbb, ti)])
                            copy("act", X[:, cc, SEG[2][0] + 16:SEG[2][0] + 48], psum[:, bank, 32:64], [PS(bank)], [(xbb, ti)])
                XK = [(xbb, "h"), (xbb, "h1"), (xbb, "h2")] + [(xbb, i) for i in range(5)]
                for si, (s0, sl) in enumerate(SEG):
                    dma("sp", pool_ov[:, 2 * g:2 * g + 2, si, :], X[:, :, s0 + sl - 16:s0 + sl], XK, [("pool_o", g, si)])
                src, srck = X, XK
                bufs = [(P1, [(p1b, 0)]), (P2, [(p2b, 0)])]
                for lev in range(g + 1):
                    sh = 2 ** lev
                    dst, dstk = bufs[lev % 2]
                    for (s0, sl) in SEG:
                        tt("pool", dst[:, :, s0 + sh:s0 + sl], src[:, :, s0 + sh:s0 + sl], src[:, :, s0:s0 + sl - sh], ALU.add, srck, dstk)
                    src, srck = dst, dstk
                for si, (s0, sl) in enumerate(SEG):
                    tc0 = 0 if si == 0 else (TP + 32 * (si - 1))
                    n = sl - 16
                    stt("dve", pooled[:, :, tc0:tc0 + n], src[:, :, s0 + 16:s0 + sl], 1.0 / win, X[:, :, s0 + 16:s0 + sl], ALU.mult, ALU.subtract,
                        srck + XK, [(pob, si)])
                tmp16 = P1[:, :, 0:16] if (g % 2 == 1) else P2[:, :, 0:16]
                tmpk = [(p1b, 0)] if (g % 2 == 1) else [(p2b, 0)]
                tt("pool", tmp16, src[:, :, 16:32], icnt_s[:, g, :].unsqueeze(1).to_broadcast([128, 2, 16]), ALU.mult,
                   srck + [CK], tmpk)
                tt("pool", pooled[:, :, 0:16], tmp16, X[:, :, 16:32], ALU.subtract, tmpk + XK, [(pob, 0)])
                PK = [(pob, i) for i in range(3)]
                for dd in range(2):
                    for ti, (c0, w) in enumerate(TILES):
                        bank = next_bank()
                        for cc in range(2):
                            mm(psum[:, bank, 0:w], wp[:, g, cc, dd * 128:(dd + 1) * 128], pooled[:, cc, c0:c0 + w], cc == 0, cc == 1,
                               [(wpb, 0)] + PK, [PS(bank)])
                        ts("dve", ybst[:, dd, c0:c0 + w], psum[:, bank, 0:w], pscale_s[:, 2 * g + dd:2 * g + dd + 1], None, ALU.mult, None,
                           [PS(bank), CK], [(ybs, 0)])
                dma("sp", ybT_d[g * 256:(g + 1) * 256, :].rearrange("(c p) t -> p c t", p=128), ybst, [(ybs, 0)], [("ybT", g)])
            for b in (xbb, p1b, p2b, pob, ybs, wpb, hhb):
                P.free(b)

            ubuf = P.alloc("uT", 8 * T * 2)
            uT = view(ubuf, BF16, [8, T])
            for blk in range(2):
                slot = load_w16(wv[:, :, blk * 512:(blk + 1) * 512], 512)
                for n in range(4):
                    for ti, (c0, w) in enumerate(TILES):
                        bank = next_bank()
                        for k in range(16):
                            mm(psum[:, bank, 0:w], W_.slots[slot][:, k, n * 128:(n + 1) * 128], hT[:, k, c0:c0 + w], k == 0, k == 15,
                               [(W_.buf, slot), HK[ti]], [PS(bank)])
                        act(uT[:, blk * 4 + n, c0:c0 + w], psum[:, bank, 0:w], AF.Gelu_apprx_tanh, [PS(bank)], [(ubuf, blk * 4 + n, ti)])

            l0c = P.alloc("l0c", 2 * 4096 + 2048 + 1024 + 4096 + 2048)
            base4 = l0c.start // 4
            lng = arena[:, base4:base4 + 1024]
            lnb = arena[:, base4 + 1024:base4 + 2048]
            wsT = arena[:, base4 + 2048:base4 + 2048 + 512].bitcast(BF16).rearrange("p (g t) -> p g t", g=8)
            wsTs = arena[:, base4 + 2560:base4 + 2560 + 256].bitcast(BF16).rearrange("p (g t) -> p g t", g=8)
            bsbc = arena[:, base4 + 2816:base4 + 2816 + 1024].rearrange("p (g t) -> p g t", g=8)
            bsbcs = arena[:, base4 + 3840:base4 + 3840 + 512].rearrange("p (g t) -> p g t", g=8)
            L0K = (l0c, 0)
            wstmp = P.alloc("wstmp", 4096 + 2048)
            wst_f = view(wstmp, F32, [8, 128])
            wsts_f = arena[:, (wstmp.start + 4096) // 4:(wstmp.start + 4096) // 4 + 512].rearrange("p (g t) -> p g t", g=8)
            dma("sp", lng, lng_in, (), [L0K])
            dma("sp", lnb, lnb_in, (), [L0K])
            dma("sp", bsbc, bsbc_in, (), [L0K])
            dma("sp", bsbcs, bsbcs_in, (), [L0K])
            dma("sp", wst_f, wsT_in, (), [(wstmp, 0)])
            dma("sp", wsts_f[0:64], wsTs_in, (), [(wstmp, 1)])
            memset("pool", wst_f[64:128, :, 0:64], 0.0, [(wstmp, 0)])
            copy("pool", wsT, wst_f, [(wstmp, 0)], [L0K])
            copy("pool", wsTs[0:64], wsts_f[0:64], [(wstmp, 1)], [L0K])
            s0 = load_w16(wv[:, :, 1024:1536], 512)
            s1 = load_w16(wv[:, :, 1536:2048], 512)
            vbuf = [P.alloc("vg0", 4096), P.alloc("vg1", 4096)]
            vnbuf = [P.alloc("vn0", 4096), P.alloc("vn1", 4096)]
            vbb = [P.alloc("vb0", 2048), P.alloc("vb1", 2048)]
            stb = P.alloc("vstat", 256)
            gtb = [P.alloc("gt0", 4 * 128 * 4), P.alloc("gt1", 4 * 128 * 4)]
            stv = view(stb, F32, [64])
            NB = 17
            for bi in range(NB):
                c0 = bi * 128
                m = 128 if bi < 16 else 64
                vg = view(vbuf[bi % 2], F32, [1024])[0:m]
                vn = view(vnbuf[bi % 2], F32, [1024])[0:m]
                vb = view(vbb[bi % 2], BF16, [1024])[0:m]
                VGK, VNK, VBK = (vbuf[bi % 2], 0), (vnbuf[bi % 2], 0), (vbb[bi % 2], 0)
                sto = (bi % 2) * 32
                st = stv[0:m, sto:sto + 32]
                STK = (stb, bi % 2)
                ti = min(bi // 4, 4)
                for half, slot in ((0, s0), (1, s1)):
                    bank = next_bank()
                    for k in range(16):
                        mm(psum[0:m, bank, :], hT[:, k, c0:c0 + m], W_.slots[slot][:, k, :], k == 0, k == 15,
                           [(W_.buf, slot), HK[ti]], [PS(bank)])
                    act(vg[:, half * 512:(half + 1) * 512], psum[0:m, bank, :], AF.Gelu_apprx_tanh, [PS(bank)], [VGK])
                for half in range(2):
                    P.emit("dve", lambda e, o=st[:, half * 6:half * 6 + 6], i=vg[:, half * 512:(half + 1) * 512]: e.bn_stats(out=o, in_=i), [VGK], [STK])
                P.emit("dve", lambda e, o=st[:, 12:14], i=st[:, 0:12].rearrange("p (a b) -> p a b", a=2): e.bn_aggr(out=o, in_=i), [STK], [STK])
                act(st[:, 14:15], st[:, 13:14], AF.Sqrt, [STK], [STK], bias=eps_ap[0:m], scale=1.0)
                P.emit("dve", lambda e, o=st[:, 15:16], i=st[:, 14:15]: e.reciprocal(out=o, in_=i), [STK], [STK])
                ts("dve", vn, vg, st[:, 12:13], st[:, 15:16], ALU.subtract, ALU.mult, [VGK, STK], [VNK])
                tt("pool", vn, vn, lng[0:m], ALU.mult, [VNK, L0K], [VNK])
                if bi < 16:
                    tt("pool", vb, vn, lnb[0:m], ALU.add, [VNK, L0K], [VBK])
                else:
                    tt("pool", vn, vn, lnb[0:m], ALU.add, [VNK, L0K], [VNK])
                    copy("pool", vb, vn, [VNK], [VBK])
                    dma("sp", vs_out, vn, [VNK], [("vs_o",)])
                for gh in range(2):
                    bank = next_bank()
                    pv = psum[:, bank, :].rearrange("p (g t) -> p g t", g=4)
                    for gi in range(4):
                        g = gh * 4 + gi
                        rhs = wsT[:, g, :] if bi < 16 else wsTs[0:64, g, :]
                        mm(pv[:, gi, 0:m], vb[:, g * 128:(g + 1) * 128], rhs, True, True, [VBK, L0K], [PS(bank)])
                    gt = view(gtb[gh], F32, [4, 128])
                    bs_ = bsbc[:, gh * 4:gh * 4 + 4, :] if bi < 16 else bsbcs[:, gh * 4:gh * 4 + 4, :]
                    tt("dve", gt[:, :, 0:m], pv[:, :, 0:m], bs_, ALU.add, [PS(bank), L0K], [(gtb[gh], 0)])
                    ukeys = [(ubuf, gh * 4 + gi, ti) for gi in range(4)]
                    tt("dve", uT[:, gh * 4:gh * 4 + 4, c0:c0 + m], gt[:, :, 0:m], uT[:, gh * 4:gh * 4 + 4, c0:c0 + m], ALU.mult,
                       [(gtb[gh], 0)] + ukeys, ukeys)
            for b in vbuf + vnbuf + vbb + [stb, wstmp, l0c] + gtb:
                P.free(b)
            P.free(hbuf)

            ybuf = P.alloc("ybT", 8 * T * 2)
            ybT = view(ybuf, BF16, [8, T])
            dma("sp", ybT, ybT_d.rearrange("(c p) t -> p c t", p=128), [("ybT", g) for g in range(4)], [(ybuf, 0)])
            xrb = P.alloc("xr", 4 * 512 * 4)
            wo = w_out_ab.rearrange("(c p) n -> p c n", p=128)
            nxt = load_w16(wo[:, :, 0:512], 512)
            cnt = 0
            for blk in range(4):
                cur = nxt
                if blk < 3:
                    nxt = load_w16(wo[:, :, (blk + 1) * 512:(blk + 2) * 512], 512)
                for n4 in range(4):
                    n = blk * 4 + n4
                    for ti, (c0, w) in enumerate(TILES):
                        bank = next_bank()
                        for k in range(16):
                            src = uT[:, k, c0:c0 + w] if k < 8 else ybT[:, k - 8, c0:c0 + w]
                            rk = [(ubuf, k, ti)] if k < 8 else [(ybuf, 0)]
                            mm(psum[:, bank, 0:w], W_.slots[cur][:, k, n4 * 128:(n4 + 1) * 128], src, k == 0, k == 15, [(W_.buf, cur)] + rk, [PS(bank)])
                        resid_update(xT_in, xs, n, c0, w, bank, mod_s[0][:, 2], 0, xrb, cnt % 4)
                        cnt += 1
            for b in (ybuf, xrb, ubuf):
                P.free(b)

        def ffn(l):
            hbuf = P.alloc("hT", 16 * T * 2)
            hT = view(hbuf, BF16, [16, T])
            norm_phase(xs, hT, hbuf, Affn[l], mod_s[l][:, 3], l)
            HK = [(hbuf, i) for i in range(5)]
            wg = w_gu[l].rearrange("(c p) n -> p c n", p=128)
            gub = P.alloc("guw", 2 * 16 * 1024 * 2)
            guw = view(gub, BF16, [2, 16, 1024])
            astb = [P.alloc("ast0", T * 2), P.alloc("ast1", T * 2)]
            sgb = [P.alloc("sg0", 2048), P.alloc("sg1", 2048)]

            def load_gu(fb):
                i = fb % 2
                dma("pool", guw[:, i, :, 0:512], wg[:, :, fb * 512:(fb + 1) * 512], (), [(gub, i)])
                dma("pool", guw[:, i, :, 512:1024], wg[:, :, DFF + fb * 512:DFF + (fb + 1) * 512], (), [(gub, i)])
            load_gu(0)
            k_ = 0
            for fb in range(11):
                if fb + 1 < 11:
                    load_gu(fb + 1)
                i = fb % 2
                for f4 in range(4):
                    f = fb * 4 + f4
                    ast = view(astb[f % 2], BF16, [T])
                    for ti, (c0, w) in enumerate(TILES):
                        bg, bu = next_bank(), next_bank()
                        for k in range(16):
                            mm(psum[:, bg, 0:w], guw[:, i, k, f4 * 128:(f4 + 1) * 128], hT[:, k, c0:c0 + w], k == 0, k == 15, [(gub, i), HK[ti]], [PS(bg)])
                        for k in range(16):
                            mm(psum[:, bu, 0:w], guw[:, i, k, 512 + f4 * 128:512 + (f4 + 1) * 128], hT[:, k, c0:c0 + w], k == 0, k == 15, [(gub, i), HK[ti]], [PS(bu)])
                        sg = view(sgb[k_ % 2], F32, [512])[:, 0:w]
                        act(sg, psum[:, bg, 0:w], AF.Silu, [PS(bg)], [(sgb[k_ % 2], 0)])
                        tt("dve", ast[:, c0:c0 + w], psum[:, bu, 0:w], sg, ALU.mult, [PS(bu), (sgb[k_ % 2], 0)], [(astb[f % 2], ti)])
                        k_ += 1
                    dma("sp", aT_d[f * 128:(f + 1) * 128, :], ast, [(astb[f % 2], ti) for ti in range(5)], [("aT", f)])
            for b in [gub] + astb + sgb + [hbuf]:
                P.free(b)
            ws_close()
            wd = w_down[l].rearrange("(c p) n -> p c n", p=128)
            wdb = [P.alloc("wd0", 44 * 512 * 2), P.alloc("wd1", 44 * 512 * 2)]
            atb = [P.alloc("at0", 44 * 512 * 2), P.alloc("at1", 44 * 512 * 2)]
            xrb = P.alloc("xr", 4 * 512 * 4)
            aTv = aT_d.rearrange("(c p) t -> p c t", p=128)
            AK = [("aT", f) for f in range(44)]

            def load_wd(nb):
                wt = view(wdb[nb % 2], BF16, [44, 512])
                for q in range(4):
                    dma("pool", wt[:, q * 11:(q + 1) * 11, :], wd[:, q * 11:(q + 1) * 11, nb * 512:(nb + 1) * 512], (), [(wdb[nb % 2], q)])

            def load_at(j):
                ti = j % 5
                c0, w = TILES[ti]
                at = view(atb[j % 2], BF16, [44, 512])
                for q in range(2):
                    dma("act", at[:, q * 22:(q + 1) * 22, 0:w], aTv[:, q * 22:(q + 1) * 22, c0:c0 + w], AK, [(atb[j % 2], q)])
            load_wd(0)
            load_at(0)
            j = 0
            cnt = 0
            for nb in range(4):
                if nb + 1 < 4:
                    load_wd(nb + 1)
                wt = view(wdb[nb % 2], BF16, [44, 512])
                for ti, (c0, w) in enumerate(TILES):
                    if j + 1 < 20:
                        load_at(j + 1)
                    at = view(atb[j % 2], BF16, [44, 512])
                    for n4 in range(4):
                        n = nb * 4 + n4
                        bank = next_bank()
                        for f in range(44):
                            mm(psum[:, bank, 0:w], wt[:, f, n4 * 128:(n4 + 1) * 128], at[:, f, 0:w], f == 0, f == 43,
                               [(wdb[nb % 2], f // 11), (atb[j % 2], f // 22)], [PS(bank)])
                        resid_update(xs, xs, n, c0, w, bank, mod_s[l][:, 5], l, xrb, cnt % 4)
                        cnt += 1
                    j += 1
            for b in wdb + atb + [xrb]:
                P.free(b)
            ws_open()


        GAM = [1.0 - 2.0 ** (-5 - h) for h in range(8)]

        def transpose(out, in_, ident, reads, writes):
            return P.emit("pe", lambda e, o=out, i=in_, d=ident: e.transpose(o, i, d), reads, writes)

        def layer1_mixer():
            l = 1
            hbuf = P.alloc("hT", 16 * T * 2)
            hT = view(hbuf, BF16, [16, T])
            norm_phase(xs, hT, hbuf, Amix[1], mod_s[1][:, 0], 1)
            HK = [(hbuf, i) for i in range(5)]
            wv = w_in_c.rearrange("(c p) n -> p c n", p=128)
            tabb = P.alloc("l1tab", 2 * T * 4 + 32 * 4 + 128 * 4)
            cosT = view(tabb, F32, [2, T])[:, 0]
            sinT = view(tabb, F32, [2, T])[:, 1]
            dtab = arena[:, (tabb.start + 2 * T * 4) // 4:(tabb.start + 2 * T * 4) // 4 + 32]
            dtab2 = arena[:, (tabb.start + 2 * T * 4) // 4 + 32:(tabb.start + 2 * T * 4) // 4 + 160]
            TK = (tabb, 0)
            dma("sp", cosT, cos_in, (), [TK])
            dma("sp", sinT, sin_in, (), [TK])
            dma("sp", dtab, dtab_in, (), [TK])
            dma("sp", dtab2, dtab2_in, (), [TK])
            kd2b = P.alloc("kds2", 16 * 256 * 2)
            kd2v = kd2_d.rearrange("(b p) n -> p b n", p=128)
            stg = [P.alloc("qkst0", 2 * T * 2), P.alloc("qkst1", 2 * T * 2)]
            rtb = [P.alloc("rt0", 4 * 512 * 4), P.alloc("rt1", 4 * 512 * 4)]
            kdsb = [P.alloc("kds0", 17 * 256 * 2), P.alloc("kds1", 17 * 256 * 2)]
            kdv = kd_d.rearrange("(b p) n -> p b n", p=128)
            hcount = 0
            rc = 0
            for blk in range(8):
                slot = load_w16(wv[:, :, blk * 512:(blk + 1) * 512], 512)
                isk = blk >= 4
                ks = 0.0625 if isk else 1.0
                for hh in range(2):
                    head = (blk % 4) * 2 + hh
                    sb_ = stg[hcount % 2]
                    st_ = view(sb_, BF16, [2, T])
                    for ti, (c0, w) in enumerate(TILES):
                        b1, b2 = next_bank(), next_bank()
                        for half, bank in ((0, b1), (1, b2)):
                            n = hh * 2 + half
                            for k in range(16):
                                mm(psum[:, bank, 0:w], W_.slots[slot][:, k, n * 128:(n + 1) * 128], hT[:, k, c0:c0 + w], k == 0, k == 15,
                                   [(W_.buf, slot), HK[ti]], [PS(bank)])
                        rb = rtb[rc % 2]
                        rt = view(rb, F32, [4, 512])
                        rc += 1
                        cs, sn = cosT[:, c0:c0 + w], sinT[:, c0:c0 + w]
                        stt("dve", rt[:, 0, 0:w], psum[:, b1, 0:w], ks, cs, ALU.mult, ALU.mult, [PS(b1), TK], [(rb, 0)])
                        stt("dve", rt[:, 1, 0:w], psum[:, b2, 0:w], ks, sn, ALU.mult, ALU.mult, [PS(b2), TK], [(rb, 1)])
                        stt("dve", rt[:, 2, 0:w], psum[:, b1, 0:w], ks, sn, ALU.mult, ALU.mult, [PS(b1), TK], [(rb, 2)])
                        stt("dve", rt[:, 3, 0:w], psum[:, b2, 0:w], ks, cs, ALU.mult, ALU.mult, [PS(b2), TK], [(rb, 3)])
                        tt("pool", st_[:, 0, c0:c0 + w], rt[:, 0, 0:w], rt[:, 1, 0:w], ALU.subtract, [(rb, 0), (rb, 1)], [(sb_, ti)])
                        tt("pool", st_[:, 1, c0:c0 + w], rt[:, 2, 0:w], rt[:, 3, 0:w], ALU.add, [(rb, 2), (rb, 3)], [(sb_, ti)])
                    SK = [(sb_, ti) for ti in range(5)]
                    dst = kT_d if isk else qT_d
                    dma("sp", dst[head * 256:(head + 1) * 256, :].rearrange("(c p) t -> p c t", p=128), st_, SK, [("qk", isk, head)])
                    if isk:
                        kb_ = kdsb[head % 2]
                        kds = view(kb_, BF16, [17, 256])
                        for bi in range(17):
                            c0 = bi * 128
                            m = 128 if bi < 16 else 64
                            bank = next_bank()
                            pT = psum[:, bank, :].bitcast(BF16)[:, 0:256].rearrange("p (c d) -> p c d", c=2)
                            for dc in range(2):
                                transpose(pT[0:m, dc, :], st_[:, dc, c0:c0 + m], identb, SK + [("identb",)], [PS(bank)])
                            dcol = 8 + head if bi < 16 else 16 + head
                            ts("dve", kds[0:m, bi, :], pT[0:m].rearrange("p c d -> p (c d)"), dtab[0:m, dcol:dcol + 1], None, ALU.mult, None,
                               [PS(bank), TK], [(kb_, bi)])
                            if bi < 16:
                                ts("dve", view(kd2b, BF16, [16, 256])[:, bi, :], pT.rearrange("p c d -> p (c d)"), dtab2[:, bi * 8 + head:bi * 8 + head + 1], None,
                                   ALU.mult, None, [PS(bank), TK], [(kd2b, bi)])
                        dma("sp", kdv[:, 0:16, head * 256:(head + 1) * 256], kds[:, 0:16, :], [(kb_, bi) for bi in range(16)], [("kd", head, 0)])
                        dma("sp", kdv[0:64, 16, head * 256:(head + 1) * 256], kds[0:64, 16, :], [(kb_, 16)], [("kd", head, 1)])
                        dma("sp", kd2v[:, :, head * 256:(head + 1) * 256], view(kd2b, BF16, [16, 256]), [(kd2b, bi) for bi in range(16)], [("kd2", head)])
                    hcount += 1
            for b in stg + rtb + kdsb + [tabb, kd2b]:
                P.free(b)
            vst = [P.alloc("vst0", 17 * 512 * 2), P.alloc("vst1", 17 * 512 * 2)]
            vdv = v_d.rearrange("(b p) n -> p b n", p=128)
            sgv = sg_d.rearrange("(b p) n -> p b n", p=128)

            def a2_block(cb):
                isg = cb >= 8
                head = cb % 8
                col0 = (8192 if isg else 4096) + head * 512
                slot = load_w16(wv[:, :, col0:col0 + 512], 512)
                vb_ = vst[cb % 2]
                vs_ = view(vb_, BF16, [17, 512])
                for bi in range(17):
                    c0 = bi * 128
                    m = 128 if bi < 16 else 64
                    ti = min(bi // 4, 4)
                    bank = next_bank()
                    for k in range(16):
                        mm(psum[0:m, bank, :], hT[:, k, c0:c0 + m], W_.slots[slot][:, k, :], k == 0, k == 15, [(W_.buf, slot), HK[ti]], [PS(bank)])
                    if isg:
                        act(vs_[0:m, bi, :], psum[0:m, bank, :], AF.Silu, [PS(bank)], [(vb_, bi)])
                    else:
                        copy("dve", vs_[0:m, bi, :], psum[0:m, bank, :], [PS(bank)], [(vb_, bi)])
                dstv = sgv if isg else vdv
                dma("sp", dstv[:, 0:16, head * 512:(head + 1) * 512], vs_[:, 0:16, :], [(vb_, bi) for bi in range(16)], [("vg", isg, head, 0)])
                dma("sp", dstv[0:64, 16, head * 512:(head + 1) * 512], vs_[0:64, 16, :], [(vb_, 16)], [("vg", isg, head, 1)])
            for cb in range(8):
                a2_block(cb)

            pk = [P.alloc("pk0", 16 * 256 * 2)] * 2
            pv = [P.alloc("pv0", 16 * 512 * 2)] * 2
            slb = [P.alloc("sl0", 2 * 512 * 4), P.alloc("sl1", 2 * 512 * 4)]
            for h in range(8):
                kk = view(pk[h % 2], BF16, [16, 256])
                vv = view(pv[h % 2], BF16, [16, 512])
                Sl = view(slb[h % 2], F32, [2, 512])
                dma("sp", kk, kd2v[:, :, h * 256:(h + 1) * 256], [("kd2", h)], [(pk[h % 2], 0)])
                dma("sp", vv, vdv[:, 0:16, h * 512:(h + 1) * 512], [("vg", False, h, 0)], [(pv[h % 2], 0)])
                for dc in range(2):
                    bank = next_bank()
                    for bi in range(16):
                        mm(psum[:, bank, :], kk[:, bi, dc * 128:(dc + 1) * 128], vv[:, bi, :], bi == 0, bi == 15, [(pk[h % 2], 0), (pv[h % 2], 0)], [PS(bank)])
                    copy("dve", Sl[:, dc, :], psum[:, bank, :], [PS(bank)], [(slb[h % 2], dc)])
                dma("sp", sloc_d[h].ap().rearrange("(c p) e -> p c e", p=128), Sl, [(slb[h % 2], 0), (slb[h % 2], 1)], [("sloc", h)])
                P.emit("pool", lambda e, i=sloc_d[h], o=sall_d[h]: e.collective_compute("AllGather", ALU.bypass, replica_groups=[list(range(8))],
                                                                                 ins=[i.ap().opt()], outs=[o.ap().opt()]),
                       [("sloc", h)], [("sall", h)], dma=True, cc=True)
            for b in pk[:1] + pv[:1] + slb:
                P.free(b)
            for cb in range(8, 16):
                a2_block(cb)
            for b in vst + [hbuf]:
                P.free(b)
            ws_close()

            ctb = P.alloc("l1c", 8 * 128 * 4 + 64 * 4 + 32 * 4)
            decT = view(ctb, F32, [8, 128])
            coef = arena[:, (ctb.start + 4096) // 4:(ctb.start + 4096) // 4 + 64]
            dtab = arena[:, (ctb.start + 4096 + 256) // 4:(ctb.start + 4096 + 256) // 4 + 32]
            CTK = (ctb, 0)
            dma("sp", decT, decT_in, (), [CTK])
            dma("sp", coef, coef_in, (), [CTK])
            dma("sp", dtab, dtab_in, (), [CTK])
            sab2 = [P.alloc("sall0", 2 * 512 * 4), P.alloc("sall1", 2 * 512 * 4)]

            qb = [P.alloc("bq0", 2 * T * 2), P.alloc("bq1", 2 * T * 2)]
            kb = [P.alloc("bk0", 2 * T * 2), P.alloc("bk1", 2 * T * 2)]
            kdb = [P.alloc("bkd0", 18 * 256 * 2), P.alloc("bkd1", 18 * 256 * 2)]
            vb2 = [P.alloc("bv0", 18 * 512 * 2), P.alloc("bv1", 18 * 512 * 2)]
            sgb2 = [P.alloc("bsg0", 18 * 512 * 2)] * 2
            ogb = P.alloc("ogst", 4 * T * 2)
            ogst = view(ogb, BF16, [4, T])
            Sb2 = [P.alloc("Sst0", 3 * 2 * 512 * 4), P.alloc("Sst1", 3 * 2 * 512 * 4)]
            Sbf2 = [P.alloc("Sbf0", 3 * 2 * 512 * 2), P.alloc("Sbf1", 3 * 2 * 512 * 2)]
            tmpb = [P.alloc("rtm0", 2 * 512 * 4 + 512 * 2 + 128 * 2 + 64), P.alloc("rtm1", 2 * 512 * 4 + 512 * 2 + 128 * 2 + 64)]
            qTv = qT_d.rearrange("(h c p) t -> p h c t", h=8, p=128)
            kTv = kT_d.rearrange("(h c p) t -> p h c t", h=8, p=128)
            cnt = 0
            for h in range(8):
                i = h % 2
                qh = view(qb[i], BF16, [2, T])
                kh = view(kb[i], BF16, [2, T])
                kdh = view(kdb[i], BF16, [18, 256])
                vh = view(vb2[i], BF16, [18, 512])
                sgh = view(sgb2[i], BF16, [18, 512])
                dma("sp", qh, qTv[:, h], [("qk", False, h)], [(qb[i], 0)])
                dma("sp", kh, kTv[:, h], [("qk", True, h)], [(kb[i], 0)])
                dma("sp", kdh[:, 0:16, :], kdv[:, 0:16, h * 256:(h + 1) * 256], [("kd", h, 0)], [(kdb[i], 0)])
                dma("sp", vh[:, 0:16, :], vdv[:, 0:16, h * 512:(h + 1) * 512], [("vg", False, h, 0)], [(vb2[i], 0)])
                dma("sp", sgh[:, 0:16, :], sgv[:, 0:16, h * 512:(h + 1) * 512], [("vg", True, h, 0)], [(sgb2[i], 0)])
                for (dst_, src_, key, bk) in ((kdh, kd_d, ("kd", h, 1), kdb[i]), (vh, v_d, ("vg", False, h, 1), vb2[i]), (sgh, sg_d, ("vg", True, h, 1), sgb2[i])):
                    wd_ = 256 if src_ is kd_d else 512
                    dma("sp", dst_[0:32, 16:18, :], src_[TP:TP + 64, h * wd_:(h + 1) * wd_].rearrange("(s p) n -> p s n", p=32), [key], [(bk, 1)])
                INK = [(qb[i], 0), (kb[i], 0), (kdb[i], 0), (kdb[i], 1), (vb2[i], 0), (vb2[i], 1), (sgb2[i], 0), (sgb2[i], 1)]
                SS = [view(b_, F32, [3, 2, 512]) for b_ in Sb2]
                SBF = [view(b_, BF16, [3, 2, 512]) for b_ in Sbf2]
                sav = sall_d[h].ap().rearrange("(r c p) e -> p r c e", r=8, p=128)
                for r in range(8):
                    sb_ = sab2[r % 2]
                    sa = view(sb_, F32, [2, 512])
                    dma("sp", sa, sav[:, r], [("sall", h)], [(sb_, 0)])
                    cf = coef[:, r * 8 + h:r * 8 + h + 1]
                    if r == 0:
                        ts("dve", SS[0][:, 0], sa, cf, None, ALU.mult, None, [(sb_, 0), CTK], [(Sb2[0], 0)])
                    else:
                        stt("dve", SS[0][:, 0], sa, cf, SS[0][:, 0], ALU.mult, ALU.add, [(sb_, 0), CTK, (Sb2[0], 0)], [(Sb2[0], 0)])
                copy("act", SBF[0][:, 0], SS[0][:, 0], [(Sb2[0], 0)], [(Sbf2[0], 0)])
                for sq, par in ((0, 0), (1, 1)):
                    dma("sp", SS[par][:, 1 + sq], sret_in[sq, h].rearrange("(c p) e -> p c e", p=128), (), [(Sb2[par], 1 + sq)])
                    copy("act", SBF[par][:, 1 + sq], SS[par][:, 1 + sq], [(Sb2[par], 1 + sq)], [(Sbf2[par], 1 + sq)])
                ctx = {}

                def cparams(ci):
                    if ci < 16:
                        return 128, ci * 128, 0, GAM[h] ** 128
                    return 32, TP + 32 * (ci - 16), 1 + (ci - 16), GAM[h] ** 32

                def tviews(tb):
                    b4 = tb.start // 4
                    return (arena[:, b4:b4 + 512], arena[:, b4 + 512:b4 + 1024], arena[:, b4 + 1024:b4 + 1280].bitcast(BF16),
                            arena[:, b4 + 1280:b4 + 1344].bitcast(BF16), arena[:, b4 + 1344:b4 + 1360])

                def stageA(ci, h=h, qh=qh, kh=kh, kdh=kdh, vh=vh, INK=INK):
                    m, c0, ss_, gL = cparams(ci)
                    par, npar = ci % 2, (ci + 1) % 2
                    tb = tmpb[ci % 2]
                    tf, of, ogt, scT, stat = tviews(tb)
                    bB, bC = (0, 1) if ci % 2 == 0 else (2, 3)
                    bA, bE0, bE1 = 4, 5, 6
                    ctx[ci] = (bB, bC, tb)
                    for dc, bE in ((0, bE0), (1, bE1)):
                        mm(psum[:, bE, :], kdh[0:m, ci, dc * 128:(dc + 1) * 128], vh[0:m, ci, :], True, True, INK, [PS(bE)])
                    for dc in range(2):
                        mm(psum[0:m, bA, 0:m], kh[:, dc, c0:c0 + m], qh[:, dc, c0:c0 + m], dc == 0, dc == 1, INK, [PS(bA)])
                    for dc, bE in ((0, bE0), (1, bE1)):
                        stt("dve", SS[npar][:, ss_, dc, :], SS[par][:, ss_, dc, :], gL, psum[:, bE, :], ALU.mult, ALU.add,
                            [PS(bE), (Sb2[par], ss_)], [(Sb2[npar], ss_)])
                    tt("dve", scT[0:m, 0:m], psum[0:m, bA, 0:m], decT[0:m, h, 0:m], ALU.mult, [PS(bA), CTK], [(tb, "sc")])
                    if ci < 15:
                        copy("act", SBF[npar][:, ss_], SS[npar][:, ss_], [(Sb2[npar], ss_)], [(Sbf2[npar], ss_)])
                    elif ci == 15:
                        dma("sp", retp_out[h].rearrange("(c p) e -> p c e", p=128), SS[npar][:, 0], [(Sb2[npar], 0)], [("retp", h)])
                    else:
                        dma("sp", rets_out[ci - 16, h].rearrange("(c p) e -> p c e", p=128), SS[npar][:, ss_], [(Sb2[npar], ss_)], [("rets", h, ci)])
                    mm(psum[0:m, bB, :], scT[0:m, 0:m], vh[0:m, ci, :], True, True, [(tb, "sc")] + INK, [PS(bB)])
                    for dc in range(2):
                        mm(psum[0:m, bC, :], qh[:, dc, c0:c0 + m], SBF[par][:, ss_, dc, :], dc == 0, dc == 1, INK + [(Sbf2[par], ss_)], [PS(bC)])

                def stageB(ci, h=h, sgh=sgh, INK=INK):
                    m, c0, ss_, gL = cparams(ci)
                    bB, bC, tb = ctx[ci]
                    tf, of, ogt, scT, stat = tviews(tb)
                    bD = 7
                    copy("act", tf[0:m], psum[0:m, bB, :], [PS(bB)], [(tb, "t")])
                    stt("dve", of[0:m], psum[0:m, bC, :], dtab[0:m, h:h + 1], tf[0:m], ALU.mult, ALU.add, [PS(bC), (tb, "t"), CTK], [(tb, "o")])
                    act(tf[0:m], of[0:m], AF.Square, [(tb, "o"), (tb, "t")], [(tb, "t"), (tb, "st")], accum_out=stat[0:m, 0:1])
                    act(stat[0:m, 1:2], stat[0:m, 0:1], AF.Sqrt, [(tb, "st")], [(tb, "st")], bias=eps_ap[0:m], scale=1.0 / 512)
                    P.emit("dve", lambda e, o=stat[0:m, 2:3], i_=stat[0:m, 1:2]: e.reciprocal(out=o, in_=i_), [(tb, "st")], [(tb, "st")])
                    stt("dve", ogt[0:m], of[0:m], stat[0:m, 2:3], sgh[0:m, ci, :], ALU.mult, ALU.mult, [(tb, "o"), (tb, "st")] + INK, [(tb, "og")])
                    pT = psum[:, bD, :].bitcast(BF16)[:, 0:512].rearrange("p (c l) -> p c l", c=4)
                    for e4 in range(4):
                        transpose(pT[:, e4, 0:m], ogt[0:m, e4 * 128:(e4 + 1) * 128], identb[0:m, 0:m], [(tb, "og"), ("identb",)], [PS(bD)])
                    copy("act", ogst[:, :, c0:c0 + m], pT[:, :, 0:m], [PS(bD)], [(ogb, ci)])
                for ci in range(19):
                    if ci < 18:
                        stageA(ci)
                    if ci >= 1:
                        stageB(ci - 1)
                dma("sp", ogT_d[h * 512:(h + 1) * 512, :].rearrange("(c p) t -> p c t", p=128), ogst, [(ogb, ci) for ci in range(18)], [("ogT", h)])
            for b in qb + kb + kdb + vb2 + sgb2[:1] + [ogb, ctb] + Sb2 + Sbf2 + sab2 + tmpb:
                P.free(b)

            ob = P.alloc("ogT", 32 * T * 2)
            ogT = view(ob, BF16, [32, T])
            for q4 in range(4):
                dma("sp", ogT[:, q4 * 8:(q4 + 1) * 8, :], ogT_d[q4 * 1024:(q4 + 1) * 1024, :].rearrange("(c p) t -> p c t", p=128),
                    [("ogT", 2 * q4), ("ogT", 2 * q4 + 1)], [(ob, q4)])
            wob = [P.alloc("wo0", 32 * 256 * 2), P.alloc("wo1", 32 * 256 * 2)]
            xrb = P.alloc("xr", 4 * 512 * 4)
            wov = w_out_c.rearrange("(c p) n -> p c n", p=128)

            def load_wo(nb):
                wt = view(wob[nb % 2], BF16, [32, 256])
                for q in range(2):
                    dma("pool", wt[:, q * 16:(q + 1) * 16, :], wov[:, q * 16:(q + 1) * 16, nb * 256:(nb + 1) * 256], (), [(wob[nb % 2], q)])
            load_wo(0)
            cnt2 = 0
            for nb in range(8):
                if nb + 1 < 8:
                    load_wo(nb + 1)
                wt = view(wob[nb % 2], BF16, [32, 256])
                for n2 in range(2):
                    n = nb * 2 + n2
                    for ti, (c0, w) in enumerate(TILES):
                        bank = next_bank()
                        for k in range(32):
                            mm(psum[:, bank, 0:w], wt[:, k, n2 * 128:(n2 + 1) * 128], ogT[:, k, c0:c0 + w], k == 0, k == 31,
                               [(wob[nb % 2], k // 16), (ob, k // 8)], [PS(bank)])
                        resid_update(xs, xs, n, c0, w, bank, mod_s[1][:, 2], 1, xrb, cnt2 % 4)
                        cnt2 += 1
            for b in wob + [xrb, ob]:
                P.free(b)
            ws_open()

        def final_norm():
            ws_close()
            xv = xs.rearrange("(c p) t -> p c t", p=128)
            yv = yT_out.rearrange("(c p) t -> p c t", p=128)
            NX, NQ = 4, 3
            xb_ = [P.alloc(f"fx{i}", 16 * 256 * 4) for i in range(NX)]
            sqb_ = [P.alloc(f"fs{i}", 16 * 256 * 4) for i in range(NQ)]
            rsb_ = [P.alloc(f"frs{i}", 3 * 256 * 4) for i in range(NQ)]
            tiles = [(i * 256, 256) for i in range(8)] + [(TP, 64)]
            banks = {}

            def bufs(si):
                c0, w = tiles[si]
                xb, sqb, rsb = xb_[si % NX], sqb_[si % NQ], rsb_[si % NQ]
                return (c0, w, xb, sqb, rsb, view(rsb, F32, [3, 256]), view(xb, F32, [16, 256])[:, :, 0:w], view(sqb, F32, [16, 256])[:, :, 0:w])

            def stage0(si):
                c0, w, xb, sqb, rsb, rs, xt, sq = bufs(si)
                dma("sp", xt, xv[:, :, c0:c0 + w], [("xs", n, (c0 // 512) * 512 if c0 < TP else TP) for n in range(16)], [(xb, 0)])

            def stage1(si):
                c0, w, xb, sqb, rsb, rs, xt, sq = bufs(si)
                act(sq, xt, AF.Square, [(xb, 0)], [(sqb, 0)])
                P.emit("dve", lambda e, o=rs[:, 0, 0:w], i=sq.rearrange("p c t -> p t c"): e.tensor_reduce(out=o, in_=i, axis=AX.X, op=ALU.add),
                       [(sqb, 0)], [(rsb, 0)])
                bank = next_bank()
                banks[si] = bank
                mm(psum[:, bank, 0:w], ones_f, rs[:, 0, 0:w], True, True, [CK, (rsb, 0)], [PS(bank)])

            def stage2(si):
                c0, w, xb, sqb, rsb, rs, xt, sq = bufs(si)
                bank = banks[si]
                act(rs[:, 1, 0:w], psum[:, bank, 0:w], AF.Sqrt, [PS(bank)], [(rsb, 1)], bias=eps_ap, scale=1.0 / D)
                P.emit("dve", lambda e, o=rs[:, 2, 0:w], i=rs[:, 1, 0:w]: e.reciprocal(out=o, in_=i), [(rsb, 1)], [(rsb, 2)])
                tt("pool", sq, xt, rs[:, 2, 0:w].unsqueeze(1).to_broadcast([128, 16, w]), ALU.mult, [(xb, 0), (rsb, 2)], [(sqb, 0)])

            def stage3(si):
                c0, w, xb, sqb, rsb, rs, xt, sq = bufs(si)
                tt("dve", sq, sq, normw_s[:, 4, :].unsqueeze(2).to_broadcast([128, 16, w]), ALU.mult, [(sqb, 0), CK], [(sqb, 0)])
                dma("sp", yv[:, :, c0:c0 + w], sq, [(sqb, 0)], [("yT", si)])
            n = len(tiles)
            for si in range(min(2, n)):
                stage0(si)
            for si in range(n + 2):
                if si + 2 < n:
                    stage0(si + 2)
                if si < n:
                    stage1(si)
                if 1 <= si <= n:
                    stage2(si - 1)
                if si >= 2:
                    stage3(si - 2)
            for b in xb_ + sqb_ + rsb_:
                P.free(b)
            ws_open()

        ada_all()
        layer0_mixer()
        ffn(0)
        layer1_mixer()
        ffn(1)
        final_norm()

        print("arena peak KB", P.peak / 1024, "ops", len(P.ops), {e: len(v) for e, v in P.by_eng.items()})
        P.finalize(nc)
    return nc


POOL_WINDOWS = (2, 4, 8, 16)


def _fm(v):
    v = np.asarray(v, np.float32)
    return np.ascontiguousarray(v.reshape(-1, 128).T)


def _core_inputs(c, I, shared):
    b, j = c // 4, c % 4
    f32 = np.float32
    m = dict(shared)
    xp = I["x_prompt"][b, j * TP:(j + 1) * TP]
    xsamp = I["x_sample"][2 * c:2 * c + 2].reshape(64, D)
    m["xT"] = np.ascontiguousarray(np.concatenate([xp, xsamp], 0).T)
    xh = np.zeros((16, D), f32)
    if j > 0:
        xh[1:] = I["x_prompt"][b, j * TP - 15:j * TP]
    m["xhT"] = np.ascontiguousarray(xh.T)
    c3 = np.stack([I["c_prompt"][b], I["c_sample"][2 * c], I["c_sample"][2 * c + 1]], -1)
    m["c3"] = np.ascontiguousarray(c3.reshape(16, 128, 3).transpose(1, 0, 2))
    cf = np.zeros((128, 8), f32)
    cf[:, 0] = 1.0 if j > 0 else 0.0
    m["cflag"] = cf
    ic = np.zeros((128, 4, 16), f32)
    for g, w in enumerate(POOL_WINDOWS):
        pos = np.arange(16) + j * TP
        ic[:, g, :] = 1.0 / np.minimum(w, pos + 1)
    m["icnt"] = ic
    ph = np.zeros((128, 8, 2, 16), f32)
    for s in range(2):
        h = I["state_b_pool"][0, 2 * c + s]
        ph[:, :, s, 1:] = h.T.reshape(8, 128, 15).transpose(1, 0, 2)
    m["phist"] = ph
    m["sret"] = np.ascontiguousarray(I["state_c_ret"][0, 2 * c:2 * c + 2], f32)
    m["w_ada_sh"] = np.ascontiguousarray(I["w_ada"][:, :, c * 1536:(c + 1) * 1536], f32)
    m["bada_rows"] = np.ascontiguousarray(np.broadcast_to(np.asarray(I["b_ada"], f32)[None, :, c * 1536:(c + 1) * 1536], (18, 2, 1536)))
    sel = np.zeros((18, 4), f32)
    sel[b, 0] = 1.0
    sel[2 + 2 * c, 1] = 1.0
    sel[2 + 2 * c + 1, 2] = 1.0
    m["selT"] = sel
    pos = np.concatenate([j * TP + np.arange(TP), 2048 + np.arange(32), 2048 + np.arange(32)]).astype(np.float32)
    freq = (np.float32(10000.0) ** (-np.arange(128, dtype=np.float32) / np.float32(128))).astype(np.float32)
    ang = (pos[None, :] * freq[:, None]).astype(np.float32)
    m["cosT"] = np.cos(ang).astype(f32)
    m["sinT"] = np.sin(ang).astype(f32)
    gam = 1.0 - 2.0 ** (-5.0 - np.arange(8, dtype=np.float64))
    cf = np.zeros((128, 64), np.float64)
    for r in range(8):
        rb, rj = r // 4, r % 4
        if rb == b and rj < j:
            cf[:, r * 8:(r + 1) * 8] = gam[None, :] ** (2048.0 * (j - 1 - rj))
    m["coef"] = cf.astype(f32)
    return m


def _shared_inputs(I):
    f32 = np.float32
    m = {}
    c18 = np.concatenate([np.asarray(I["c_prompt"], f32), np.asarray(I["c_sample"], f32)], 0)
    m["c18T"] = np.ascontiguousarray(c18.reshape(18, 16, 128).transpose(2, 1, 0))
    nw = np.stack([I["norm_mix"][0], I["norm_mix"][1], I["norm_ffn"][0], I["norm_ffn"][1], I["norm_final"]], 0)
    m["normw"] = np.ascontiguousarray(np.asarray(nw, f32).reshape(5, 16, 128).transpose(2, 0, 1))
    m["w_in_ab"] = np.ascontiguousarray(I["w_in_ab"][0], f32)
    m["lng"] = np.ascontiguousarray(np.broadcast_to(np.asarray(I["ln_v_g"][0], f32), (128, 1024)))
    m["lnb"] = np.ascontiguousarray(np.broadcast_to(np.asarray(I["ln_v_b"][0], f32), (128, 1024)))
    ws = np.asarray(I["w_s"][0], f32)
    m["wsT"] = np.ascontiguousarray(ws.transpose(2, 0, 1))
    wss = np.zeros((64, 8, 64), f32)
    blk = ws[:, :32, :32].transpose(2, 0, 1)
    wss[0:32, :, 0:32] = blk
    wss[32:64, :, 32:64] = blk
    m["wsTs"] = wss
    bs = np.asarray(I["b_s"][0], f32)
    m["bsbc"] = np.ascontiguousarray(np.broadcast_to(bs, (128, 8, 128)))
    m["bsbcs"] = np.ascontiguousarray(np.broadcast_to(np.concatenate([bs[:, :32], bs[:, :32]], 1), (128, 8, 64)))
    m["w_pool"] = np.ascontiguousarray(I["w_pool"][0], f32)
    m["pscaleT"] = _fm(I["pool_scale"][0])
    m["w_out_ab"] = np.ascontiguousarray(I["w_out_ab"][0], f32)
    m["w_ffn_gu"] = np.ascontiguousarray(I["w_ffn_gu"], f32)
    m["w_ffn_down"] = np.ascontiguousarray(I["w_ffn_down"], f32)
    m["ident"] = np.eye(128, dtype=f32)
    m["w_in_c"] = np.ascontiguousarray(I["w_in_c"][0], f32)
    m["w_out_c"] = np.ascontiguousarray(I["w_out_c"][0], f32)
    gam = 1.0 - 2.0 ** (-5.0 - np.arange(8, dtype=np.float64))
    p = np.arange(128, dtype=np.float64)
    dt_ = np.zeros((128, 32), np.float64)
    dt_[:, 0:8] = gam[None, :] ** (p[:, None] + 1.0)
    dt_[:, 8:16] = gam[None, :] ** (127.0 - p[:, None])
    dt_[:, 16:24] = gam[None, :] ** (31.0 - (p[:, None] % 32))
    m["dtab"] = dt_.astype(f32)
    d2 = np.zeros((128, 128), np.float64)
    for bi in range(16):
        d2[:, bi * 8:(bi + 1) * 8] = gam[None, :] ** (2047.0 - (bi * 128.0 + p[:, None]))
    m["dtab2"] = d2.astype(f32)
    diff = p[None, :] - p[:, None]
    dec = np.where(diff[:, None, :] >= 0, gam[None, :, None] ** np.maximum(diff[:, None, :], 0.0), 0.0)
    m["decT"] = np.ascontiguousarray(dec.astype(f32))
    return m


_NC_CACHE = {}


def _run(inputs, debug=False):
    I = {k: np.asarray(v) for k, v in inputs.items()}
    if debug not in _NC_CACHE:
        _NC_CACHE[debug] = build_program(debug)
    nc = _NC_CACHE[debug]
    shared = _shared_inputs(I)
    in_maps = [_core_inputs(c, I, shared) for c in range(NCORES)]
    res = run_bass_kernel_spmd(nc, in_maps, core_ids=list(range(NCORES)))
    return res.results


def kernel(**inputs):
    R = _run(inputs)
    f32 = np.float32
    y_prompt = np.zeros((2, 8192, D), f32)
    y_sample = np.zeros((16, 32, D), f32)
    pool_p = np.zeros((1, 2, 15, 1024), f32)
    pool_s = np.zeros((1, 16, 15, 1024), f32)
    v_s = np.zeros((1, 16, 32, 1024), f32)
    ret_p = np.zeros((1, 2, 8, 256, 512), f32)
    ret_s = np.zeros((1, 16, 8, 256, 512), f32)
    for c in range(NCORES):
        b, j = c // 4, c % 4
        r = R[c]
        yT = r["yT"]
        y_prompt[b, j * TP:(j + 1) * TP] = yT[:, :TP].T
        y_sample[2 * c] = yT[:, TP:TP + 32].T
        y_sample[2 * c + 1] = yT[:, TP + 32:].T
        po = r["pool_o"]
        po = po.transpose(2, 3, 1, 0).reshape(3, 16, 1024)
        if j == 3:
            pool_p[0, b] = po[0, 1:]
        pool_s[0, 2 * c] = po[1, 1:]
        pool_s[0, 2 * c + 1] = po[2, 1:]
        v_s[0, 2 * c] = r["vs_o"][:32]
        v_s[0, 2 * c + 1] = r["vs_o"][32:]
        if j == 3:
            ret_p[0, b] = r["retp_o"]
        ret_s[0, 2 * c:2 * c + 2] = r["rets_o"]
    return (y_prompt, y_sample, pool_p, pool_s, v_s, ret_p, ret_s)
```

```python
import numpy as np
import concourse.bass as bass
import concourse.mybir as mybir
from concourse.bass_utils import run_bass_kernel_spmd

F32, BF16 = mybir.dt.float32, mybir.dt.bfloat16
AF = mybir.ActivationFunctionType
ALU = mybir.AluOpType
AX = mybir.AxisListType

NCORES = 8
D = 2048
T = 2112
TP = 2048
TILES = [(0, 512), (512, 512), (1024, 512), (1536, 512), (2048, 64)]
DFF = 5632
EPS = 1e-6
NSEM_DMA = 6
ARENA_BYTES = 198 * 1024


class Op:
    __slots__ = ("id", "eng", "fn", "deps", "dma", "sig", "idx", "dsem", "dval")


class Buf:
    def __init__(self, name, start, nbytes):
        self.name, self.start, self.nbytes = name, start, nbytes
        self.inherited = set()
        self.keys = set()


class Prog:
    ENGS = ("pe", "act", "dve", "pool", "sp")

    def __init__(self):
        self.ops = []
        self.by_eng = {e: [] for e in self.ENGS}
        self.lastw = {}
        self.readers = {}
        self.dma_n = {e: 0 for e in self.ENGS}
        self.dma_slot_last = {}
        self.free_list = [(0, ARENA_BYTES)]
        self.freed = []
        self.peak = 0
        self.live = {}

    def alloc(self, name, nbytes):
        nbytes = (nbytes + 63) // 64 * 64
        for i, (s, e) in enumerate(self.free_list):
            if e - s >= nbytes:
                self.free_list[i] = (s + nbytes, e)
                b = Buf(name, s, nbytes)
                for (fs, fe, opset) in self.freed:
                    if fs < s + nbytes and s < fe:
                        b.inherited |= opset
                self.live[name] = b
                self.peak = max(self.peak, max(x.start + x.nbytes for x in self.live.values()))
                return b
        raise RuntimeError(f"arena full allocating {name} {nbytes}: {self.free_list} live={[(k, v.nbytes) for k, v in self.live.items()]}")

    def free(self, b):
        opset = set(b.inherited)
        for k in b.keys:
            if self.lastw.get(k) is not None:
                opset.add(self.lastw[k])
            opset |= set(self.readers.get(k, ()))
            self.lastw.pop(k, None)
            self.readers.pop(k, None)
        opset = self._compress(opset)
        self.freed = [(fs, fe, o) for (fs, fe, o) in self.freed if not (fs >= b.start and fe <= b.start + b.nbytes)]
        self.freed.append((b.start, b.start + b.nbytes, opset))
        del self.live[b.name]
        fl = self.free_list + [(b.start, b.start + b.nbytes)]
        fl.sort()
        merged = []
        for s, e in fl:
            if s == e:
                continue
            if merged and merged[-1][1] == s:
                merged[-1] = (merged[-1][0], e)
            else:
                merged.append((s, e))
        self.free_list = merged

    def _compress(self, opset):
        best = {}
        for i in opset:
            o = self.ops[i]
            k = (o.eng, o.dsem) if o.dma else (o.eng, None)
            if k not in best or best[k] < i:
                best[k] = i
        return set(best.values())

    def emit(self, eng, fn, reads=(), writes=(), dma=False, cc=False):
        o = Op()
        o.id = len(self.ops)
        o.eng, o.fn, o.dma = eng, fn, dma
        o.sig, o.idx, o.dsem, o.dval = False, 0, None, 0
        deps = set()
        for k in list(reads) + list(writes):
            if isinstance(k, tuple) and isinstance(k[0], Buf):
                b = k[0]
                if k not in b.keys:
                    b.keys.add(k)
                    deps |= b.inherited
        for k in reads:
            w = self.lastw.get(k)
            if w is not None:
                deps.add(w)
        for k in writes:
            w = self.lastw.get(k)
            if w is not None:
                deps.add(w)
            deps |= set(self.readers.get(k, ()))
        if dma:
            n = self.dma_n[eng]
            self.dma_n[eng] = n + 1
            slot = n % NSEM_DMA
            o.dsem = (eng, "cc") if cc else (eng, slot)
            inc = 1 if cc else 16
            prev = self.dma_slot_last.get(o.dsem)
            if prev is not None:
                if not cc:
                    deps.add(prev)
                o.dval = self.ops[prev].dval + inc
            else:
                o.dval = inc
            self.dma_slot_last[o.dsem] = o.id
            o.sig = True
        o.deps = deps
        self.ops.append(o)
        self.by_eng[eng].append(o)
        for k in reads:
            lst = self.readers.setdefault(k, [])
            if not dma:
                lst[:] = [r for r in lst if self.ops[r].dma or self.ops[r].eng != eng]
            lst.append(o.id)
        for k in writes:
            self.lastw[k] = o.id
            self.readers[k] = []
        return o

    def finalize(self, nc):
        for o in self.ops:
            for d in o.deps:
                p = self.ops[d]
                if not p.dma and not (p.eng == "pe" and o.eng == "pe" and not o.dma):
                    p.sig = True
        for e in self.ENGS:
            n = 0
            for o in self.by_eng[e]:
                if not o.dma and o.sig:
                    n += 1
                    o.idx = n
        import contextlib
        with contextlib.ExitStack() as st:
            esem = {e: st.enter_context(nc.semaphore("s_" + e)) for e in self.ENGS}
            dsem = {}
            for e in self.ENGS:
                if self.dma_n[e]:
                    for s in range(NSEM_DMA):
                        dsem[(e, s)] = st.enter_context(nc.semaphore(f"d_{e}{s}"))
                    dsem[(e, "cc")] = st.enter_context(nc.semaphore(f"c_{e}"))
            fin = st.enter_context(nc.semaphore("fin"))
            block = st.enter_context(nc.Block())
            prog = self

            def run(eng_name, e):
                waited = {}
                for o in prog.by_eng[eng_name]:
                    need = {}
                    for d in o.deps:
                        p = prog.ops[d]
                        if p.dma:
                            sem, val = dsem[p.dsem], p.dval
                        else:
                            if p.eng == "pe" and eng_name == "pe" and not o.dma:
                                continue
                            sem, val = esem[p.eng], p.idx
                        key = id(sem)
                        if waited.get(key, 0) >= val:
                            continue
                        if key not in need or need[key][1] < val:
                            need[key] = (sem, val)
                    for key, (sem, val) in need.items():
                        e.wait_ge(sem, val)
                        waited[key] = val
                    ins = o.fn(e)
                    if o.dma and o.dsem[1] == "cc":
                        ins.then_inc(dsem[o.dsem])
                    elif o.dma:
                        ins.then_inc(dsem[o.dsem], 16)
                    elif o.sig:
                        ins.then_inc(esem[eng_name], 1)
                last_sig = None
                return

            finals_d = {}
            for o in prog.ops:
                if o.dma:
                    finals_d[o.dsem] = max(finals_d.get(o.dsem, 0), o.dval)
            finals_e = {e: max([o.idx for o in prog.by_eng[e] if not o.dma] + [0]) for e in self.ENGS}

            @block.tensor
            def _(e):
                run("pe", e)

            @block.scalar
            def _(e):
                run("act", e)

            @block.vector
            def _(e):
                run("dve", e)

            @block.gpsimd
            def _(e):
                run("pool", e)

            @block.sync
            def _(e):
                run("sp", e)
                for k, v in finals_d.items():
                    e.wait_ge(dsem[k], v)
                for en, v in finals_e.items():
                    if v:
                        e.wait_ge(esem[en], v)


def build_program(debug=False):
    nc = bass.Bass("TRN2", target_bir_lowering=False)
    P = Prog()

    def din(name, shape, dt=F32):
        return nc.dram_tensor(name, list(shape), dt, kind="ExternalInput").ap()

    def dout(name, shape, dt=F32):
        return nc.dram_tensor(name, list(shape), dt, kind="ExternalOutput").ap()

    def dscr(name, shape, dt):
        return nc.dram_tensor(name, list(shape), dt).ap()

    xT_in = din("xT", [D, T])
    xhT_in = din("xhT", [D, 16])
    c3_in = din("c3", [128, 16, 3])
    cflag_in = din("cflag", [128, 8])
    icnt_in = din("icnt", [128, 4, 16])
    phist_in = din("phist", [128, 8, 2, 16])
    w_ada = din("w_ada_sh", [2, D, 1536])
    bada_rows = din("bada_rows", [18, 2, 1536])
    c18_in = din("c18T", [128, 16, 18])
    selT_in = din("selT", [18, 4])
    normw = din("normw", [128, 5, 16])
    w_in_ab = din("w_in_ab", [D, 3072])
    lng_in = din("lng", [128, 1024])
    lnb_in = din("lnb", [128, 1024])
    wsT_in = din("wsT", [128, 8, 128])
    wsTs_in = din("wsTs", [64, 8, 64])
    bsbc_in = din("bsbc", [128, 8, 128])
    bsbcs_in = din("bsbcs", [128, 8, 64])
    w_pool = din("w_pool", [4, 256, 256])
    pscale_in = din("pscaleT", [128, 8])
    w_out_ab = din("w_out_ab", [D, D])
    w_gu = din("w_ffn_gu", [2, D, 2 * DFF])
    w_down = din("w_ffn_down", [2, DFF, D])
    ident_in = din("ident", [128, 128])
    w_in_c = din("w_in_c", [D, 12288])
    w_out_c = din("w_out_c", [4096, D])
    sret_in = din("sret", [2, 8, 256, 512])
    cos_in = din("cosT", [128, T])
    sin_in = din("sinT", [128, T])
    dtab_in = din("dtab", [128, 32])
    decT_in = din("decT", [128, 8, 128])
    coef_in = din("coef", [128, 64])
    dtab2_in = din("dtab2", [128, 128])

    yT_out = dout("yT", [D, T])
    pool_out = dout("pool_o", [128, 8, 3, 16])
    vs_out = dout("vs_o", [64, 1024])
    retp_out = dout("retp_o", [8, 256, 512])
    rets_out = dout("rets_o", [2, 8, 256, 512])

    xs = dscr("xs", [D, T], F32)
    ybT_d = dscr("ybT", [1024, T], BF16)
    aT_d = dscr("aT", [DFF, T], BF16)
    qT_d = dscr("qT", [D, T], BF16)
    kT_d = dscr("kT", [D, T], BF16)
    kd_d = dscr("kd", [17 * 128, D], BF16)
    kd2_d = dscr("kd2", [16 * 128, D], BF16)
    v_d = dscr("vtok", [17 * 128, 4096], BF16)
    sg_d = dscr("sgtok", [17 * 128, 4096], BF16)
    ogT_d = dscr("ogT", [4096, T], BF16)
    sloc_d = [nc.dram_tensor(f"sloc{h}", [256, 512], F32) for h in range(8)]
    mloc_d = nc.dram_tensor("mloc", [18, 3072], F32)
    mall_d = nc.dram_tensor("mall", [8 * 18, 3072], F32)
    sall_d = [nc.dram_tensor(f"sall{h}", [8 * 256, 512], F32) for h in range(8)]

    dbg = {}
    if debug:
        dbg["x_l0"] = dout("dbg_x_l0", [D, T])

    import contextlib
    with contextlib.ExitStack() as stack:
        arena = stack.enter_context(nc.sbuf_tensor("arena", [128, ARENA_BYTES // 4], F32))
        cst = stack.enter_context(nc.sbuf_tensor("cst", [128, 2048], F32))
        psum = stack.enter_context(nc.psum_tensor("psum", [128, 8, 512], F32))

        def view(buf, dt, shape):
            n4 = buf.nbytes // 4
            ap = arena[:, buf.start // 4: buf.start // 4 + n4]
            if dt != F32:
                ap = ap.bitcast(dt)
            esz = 4 if dt == F32 else 2
            tot = 1
            for s in shape:
                tot *= s
            assert tot * esz <= buf.nbytes, (buf.name, shape, buf.nbytes)
            ap = ap[:, 0:tot]
            if len(shape) == 1:
                return ap
            names = " ".join(f"a{i}" for i in range(len(shape)))
            kw = {f"a{i}": s for i, s in enumerate(shape[:-1])}
            return ap.rearrange(f"p ({names}) -> p {names}", **kw)

        _cpos = [0]

        def cst_alloc(n):
            s = _cpos[0]
            _cpos[0] += n
            assert _cpos[0] <= 2048
            return cst[:, s:s + n]

        ident_f = cst_alloc(128)
        ones_f = cst_alloc(128)
        normw_s = cst_alloc(80).rearrange("p (a c) -> p a c", a=5)
        cflag_s = cst_alloc(8)
        mod_s = [cst_alloc(288).rearrange("p (v c r) -> p v c r", v=6, c=16) for _ in range(2)]
        Amix = [cst_alloc(48).rearrange("p (c r) -> p c r", c=16) for _ in range(2)]
        Affn = [cst_alloc(48).rearrange("p (c r) -> p c r", c=16) for _ in range(2)]
        bada_s = cst_alloc(192).rearrange("p (l c) -> p l c", l=2)
        pscale_s = cst_alloc(8)
        icnt_s = cst_alloc(64).rearrange("p (g t) -> p g t", g=4)
        c3_s = cst_alloc(48).rearrange("p (c r) -> p c r", c=16)
        stat_s = cst_alloc(64)
        identb = cst_alloc(64).bitcast(BF16)
        sc_b = cst_alloc(24).bitcast(BF16).rearrange("p (c r) -> p c r", c=16)

        CK = ("cst",)

        def PS(b):
            return ("ps", b)

        ps_rr = [0]

        def next_bank():
            b = ps_rr[0] % 8
            ps_rr[0] += 1
            return b

        def dma(q, out, in_, reads, writes):
            return P.emit(q, lambda e, o=out, i=in_: e.dma_start(out=o, in_=i), reads, writes, dma=True)

        def act(out, in_, func, reads, writes, bias=None, scale=None, accum_out=None):
            kw = {}
            if bias is not None:
                kw["bias"] = bias
            if scale is not None:
                kw["scale"] = scale
            if accum_out is not None:
                kw["accum_out"] = accum_out
            return P.emit("act", lambda e, o=out, i=in_, f=func, kw=kw: e.activation(out=o, in_=i, func=f, **kw), reads, writes)

        def tt(eng, out, in0, in1, op, reads, writes):
            return P.emit(eng, lambda e, o=out, a=in0, b=in1, op=op: e.tensor_tensor(out=o, in0=a, in1=b, op=op), reads, writes)

        def ts(eng, out, in0, s1, s2, op0, op1, reads, writes):
            if op1 is None:
                return P.emit(eng, lambda e, o=out, a=in0, s1=s1, op0=op0: e.tensor_scalar(out=o, in0=a, scalar1=s1, scalar2=None, op0=op0), reads, writes)
            return P.emit(eng, lambda e, o=out, a=in0, s1=s1, s2=s2, op0=op0, op1=op1: e.tensor_scalar(out=o, in0=a, scalar1=s1, scalar2=s2, op0=op0, op1=op1), reads, writes)

        def stt(eng, out, in0, scalar, in1, op0, op1, reads, writes):
            return P.emit(eng, lambda e, o=out, a=in0, s=scalar, b=in1, op0=op0, op1=op1: e.scalar_tensor_tensor(out=o, in0=a, scalar=s, in1=b, op0=op0, op1=op1), reads, writes)

        def copy(eng, out, in_, reads, writes):
            if eng == "act":
                return P.emit("act", lambda e, o=out, i=in_: e.copy(out=o, in_=i), reads, writes)
            return P.emit(eng, lambda e, o=out, i=in_: e.tensor_copy(out=o, in_=i), reads, writes)

        def mm(out, lhsT, rhs, start, stop, reads, writes):
            return P.emit("pe", lambda e, o=out, l=lhsT, r=rhs, s=start, t=stop: e.matmul(o, lhsT=l, rhs=r, start=s, stop=t), reads, writes)

        def memset(eng, ap, val, writes):
            return P.emit(eng, lambda e, a=ap, v=val: e.memset(a, v), (), writes)

        dma("sp", ident_f, ident_in, (), [CK])
        dma("sp", normw_s, normw, (), [CK])
        dma("sp", cflag_s, cflag_in, (), [CK])
        dma("sp", pscale_s, pscale_in, (), [CK])
        dma("sp", icnt_s, icnt_in, (), [CK])
        dma("sp", c3_s, c3_in, (), [CK])
        memset("dve", ones_f, 1.0, [CK])
        copy("dve", identb, ident_f, [CK], [("identb",)])

        class _W:
            pass
        W_ = _W()

        def ws_open():
            W_.buf = P.alloc("wslots", 3 * 16384)
            W_.slots = [view(W_.buf, BF16, [3, 16, 512])[:, i] for i in range(3)]

        def ws_close():
            P.free(W_.buf)
        ws_open()
        wrr = [0]

        def wslot_next():
            i = wrr[0] % 3
            wrr[0] += 1
            return i

        def load_w16(src3d, ncols):
            i = wslot_next()
            dma("pool", W_.slots[i][:, :, 0:ncols], src3d, (), [(W_.buf, i)])
            return i

        def ada_all():
            ab = P.alloc("adatmp", 8 * 3072 * 4)
            mall = view(ab, F32, [8, 3072])
            cb = P.alloc("adac", 16 * 18 * 4 + 16 * 18 * 2 + 2 * 1536 * 4 + 2 * 1536 * 4 + 64)
            b4 = cb.start // 4
            c18 = arena[:, b4:b4 + 288].rearrange("p (c q) -> p c q", c=16)
            sc18 = arena[:, b4 + 288:b4 + 432].bitcast(BF16).rearrange("p (c q) -> p c q", c=16)
            brow = arena[:, b4 + 432:b4 + 432 + 3072].rearrange("p (l n) -> p l n", l=2)
            mp = arena[:, b4 + 3504:b4 + 3504 + 3072].rearrange("p (l n) -> p l n", l=2)
            selT = arena[:, b4 + 6576:b4 + 6580]
            AK_ = (cb, 0)
            dma("sp", c18, c18_in, (), [AK_])
            dma("sp", brow[0:18], bada_rows, (), [AK_])
            dma("sp", selT[0:18], selT_in, (), [AK_])
            act(sc18, c18, AF.Silu, [AK_], [(cb, 1)])
            for l in range(2):
                wv = w_ada[l].rearrange("(c p) n -> p c n", p=128)
                for blk in range(3):
                    slot = load_w16(wv[:, :, blk * 512:(blk + 1) * 512], 512)
                    bank = next_bank()
                    for k in range(16):
                        mm(psum[0:18, bank, :], sc18[:, k, :], W_.slots[slot][:, k, :], k == 0, k == 15, [(W_.buf, slot), (cb, 1)], [PS(bank)])
                    tt("dve", mp[0:18, l, blk * 512:(blk + 1) * 512], psum[0:18, bank, :], brow[0:18, l, blk * 512:(blk + 1) * 512], ALU.add,
                       [PS(bank), AK_], [(cb, 2)])
            dma("sp", mloc_d.ap(), mp[0:18].rearrange("p l n -> p (l n)"), [(cb, 2)], [("mloc",)])
            P.emit("pool", lambda e: e.collective_compute("AllGather", ALU.bypass, replica_groups=[list(range(8))],
                                                         ins=[mloc_d.ap().opt()], outs=[mall_d.ap().opt()]),
                   [("mloc",)], [("mall",)], dma=True, cc=True)
            dma("sp", mall[0:18], mall_d.ap().rearrange("(r q) c -> q r c", q=18), [("mall",)], [(ab, 0)])
            for l in range(2):
                bank = next_bank()
                pst = psum[:, bank, 0:288].rearrange("p (j r) -> p j r", r=3)
                for r in range(8):
                    for n in range(12):
                        j = r * 12 + n
                        mm(pst[:, j, :], mall[0:18, r, l * 1536 + n * 128:l * 1536 + (n + 1) * 128], selT[0:18, 0:3], True, True,
                           [(ab, 0), AK_], [PS(bank)])
                mflat = mod_s[l].rearrange("p v c r -> p (v c) r")
                copy("dve", mflat, pst, [PS(bank)], [("mod", l)])
                for (Adst, vi, nw) in ((Amix[l], 1, l), (Affn[l], 4, 2 + l)):
                    stt("dve", Adst, mod_s[l][:, vi], 1.0, normw_s[:, nw, :].unsqueeze(2).to_broadcast([128, 16, 3]),
                        ALU.add, ALU.mult, [("mod", l), CK], [("modA", l)])
            P.free(ab)
            P.free(cb)

        def colr(c0, w):
            if c0 < TP:
                return [(0, w, 0)]
            return [(0, 32, 1), (32, 32, 2)]

        def norm_phase(xsrc, hT, hbuf, A, Bmod, l, from_xs=True):
            ws_close()
            xv = xsrc.rearrange("(c p) t -> p c t", p=128)
            NX, NQ = 4, 3
            xb_ = [P.alloc(f"nx{i}", 16 * 256 * 4) for i in range(NX)]
            sqb_ = [P.alloc(f"nsq{i}", 16 * 256 * 4) for i in range(NQ)]
            rsb_ = [P.alloc(f"nrs{i}", 3 * 256 * 4) for i in range(NQ)]
            tiles = [(i * 256, 256) for i in range(8)] + [(TP, 64)]
            banks = {}

            def bufs(si):
                c0, w = tiles[si]
                xb, sqb, rsb = xb_[si % NX], sqb_[si % NQ], rsb_[si % NQ]
                return (c0, w, xb, sqb, rsb, view(rsb, F32, [3, 256]), view(xb, F32, [16, 256])[:, :, 0:w], view(sqb, F32, [16, 256])[:, :, 0:w])

            def stage0(si):
                c0, w, xb, sqb, rsb, rs, xt, sq = bufs(si)
                dma("sp", xt, xv[:, :, c0:c0 + w], [("xs", n, (c0 // 512) * 512 if c0 < TP else TP) for n in range(16)] if from_xs else (), [(xb, 0)])

            def stage1(si):
                c0, w, xb, sqb, rsb, rs, xt, sq = bufs(si)
                act(sq, xt, AF.Square, [(xb, 0)], [(sqb, 0)])
                P.emit("dve", lambda e, o=rs[:, 0, 0:w], i=sq.rearrange("p c t -> p t c"): e.tensor_reduce(out=o, in_=i, axis=AX.X, op=ALU.add),
                       [(sqb, 0)], [(rsb, 0)])
                bank = next_bank()
                banks[si] = bank
                mm(psum[:, bank, 0:w], ones_f, rs[:, 0, 0:w], True, True, [CK, (rsb, 0)], [PS(bank)])

            def stage2(si):
                c0, w, xb, sqb, rsb, rs, xt, sq = bufs(si)
                bank = banks[si]
                act(rs[:, 1, 0:w], psum[:, bank, 0:w], AF.Sqrt, [PS(bank)], [(rsb, 1)], bias=eps_ap, scale=1.0 / D)
                P.emit("dve", lambda e, o=rs[:, 2, 0:w], i=rs[:, 1, 0:w]: e.reciprocal(out=o, in_=i), [(rsb, 1)], [(rsb, 2)])
                tt("pool", sq, xt, rs[:, 2, 0:w].unsqueeze(1).to_broadcast([128, 16, w]), ALU.mult, [(xb, 0), (rsb, 2)], [(sqb, 0)])

            def stage3(si):
                c0, w, xb, sqb, rsb, rs, xt, sq = bufs(si)
                ti = min(c0 // 512, 4)
                for c in range(16):
                    for (o0, ww, r) in colr(c0, w):
                        if c % 4 == 3:
                            ts("dve", hT[:, c, c0 + o0:c0 + o0 + ww], sq[:, c, o0:o0 + ww], A[:, c, r:r + 1], Bmod[:, c, r:r + 1], ALU.mult, ALU.add,
                               [(sqb, 0), ("modA", l), ("mod", l)], [(hbuf, ti)])
                        else:
                            act(hT[:, c, c0 + o0:c0 + o0 + ww], sq[:, c, o0:o0 + ww], AF.Identity, [(sqb, 0), ("modA", l), ("mod", l)],
                                [(hbuf, ti)], bias=Bmod[:, c, r:r + 1], scale=A[:, c, r:r + 1])
            n = len(tiles)
            for si in range(min(2, n)):
                stage0(si)
            for si in range(n + 2):
                if si + 2 < n:
                    stage0(si + 2)
                if si < n:
                    stage1(si)
                if 1 <= si <= n:
                    stage2(si - 1)
                if si >= 2:
                    stage3(si - 2)
            for b in xb_ + sqb_ + rsb_:
                P.free(b)
            ws_open()

        eps_ap = cst_alloc(1)
        memset("dve", eps_ap, EPS, [CK])

        def resid_update(xsrc, xdst, n, c0, w, bank, gate, l, xrb, slot):
            xt = view(xrb, F32, [4, 512])[:, slot, 0:w]
            dma("sp", xt, xsrc[n * 128:(n + 1) * 128, c0:c0 + w], [("xs", n, c0)], [(xrb, slot)])
            for (o0, ww, r) in colr(c0, w):
                stt("dve", xt[:, o0:o0 + ww], psum[:, bank, o0:o0 + ww], gate[:, n, r:r + 1], xt[:, o0:o0 + ww], ALU.mult, ALU.add,
                    [PS(bank), (xrb, slot), ("mod", l)], [(xrb, slot)])
            dma("sp", xdst[n * 128:(n + 1) * 128, c0:c0 + w], xt, [(xrb, slot)], [("xs", n, c0)])

        def layer0_mixer():
            l = 0
            hbuf = P.alloc("hT", 16 * T * 2)
            hT = view(hbuf, BF16, [16, T])
            norm_phase(xT_in, hT, hbuf, Amix[0], mod_s[0][:, 0], 0, from_xs=False)
            HK = [(hbuf, i) for i in range(5)]
            hhb = P.alloc("hh", 16 * 16 * 2 + 16 * 16 * 4 * 2 + 3 * 16 * 4)
            hh = view(hhb, BF16, [16, 16])
            hx = arena[:, (hhb.start + 512) // 4:(hhb.start + 512) // 4 + 256].rearrange("p (c t) -> p c t", c=16)
            hq = arena[:, (hhb.start + 512 + 1024) // 4:(hhb.start + 512 + 1024) // 4 + 256].rearrange("p (c t) -> p c t", c=16)
            hr = arena[:, (hhb.start + 512 + 2048) // 4:(hhb.start + 512 + 2048) // 4 + 48].rearrange("p (c t) -> p c t", c=3)
            dma("sp", hx, xhT_in.rearrange("(c p) t -> p c t", p=128), (), [(hhb, 0)])
            act(hq, hx, AF.Square, [(hhb, 0)], [(hhb, 1)])
            P.emit("dve", lambda e, o=hr[:, 0, :], i=hq.rearrange("p c t -> p t c"): e.tensor_reduce(out=o, in_=i, axis=AX.X, op=ALU.add), [(hhb, 1)], [(hhb, 2)])
            bank = next_bank()
            mm(psum[:, bank, 0:16], ones_f, hr[:, 0, :], True, True, [CK, (hhb, 2)], [PS(bank)])
            act(hr[:, 1, :], psum[:, bank, 0:16], AF.Sqrt, [PS(bank)], [(hhb, 3)], bias=eps_ap, scale=1.0 / D)
            P.emit("dve", lambda e, o=hr[:, 2, :], i=hr[:, 1, :]: e.reciprocal(out=o, in_=i), [(hhb, 3)], [(hhb, 4)])
            tt("dve", hq, hx, hr[:, 2, :].unsqueeze(1).to_broadcast([128, 16, 16]), ALU.mult, [(hhb, 0), (hhb, 4)], [(hhb, 1)])
            for c in range(16):
                act(hh[:, c, :], hq[:, c, :], AF.Identity, [(hhb, 1), ("modA", 0), ("mod", 0)], [(hhb, 5)],
                    bias=mod_s[0][:, 0, c, 0:1], scale=Amix[0][:, c, 0:1])

            wv = w_in_ab.rearrange("(c p) n -> p c n", p=128)

            SEG = [(0, 16 + TP), (16 + TP, 48), (16 + TP + 48, 48)]
            XL = 16 + TP + 96
            xbb = P.alloc("xb", 2 * XL * 4)
            p1b = P.alloc("pp1", 2 * XL * 4)
            p2b = P.alloc("pp2", 2 * XL * 4)
            pob = P.alloc("pooled", 2 * T * 2)
            ybs = P.alloc("ybstage", 2 * T * 2)
            wpb = P.alloc("wpool", 4 * 2 * 256 * 2)
            X = view(xbb, F32, [2, XL])
            P1 = view(p1b, F32, [2, XL])
            P2 = view(p2b, F32, [2, XL])
            pooled = view(pob, BF16, [2, T])
            ybst = view(ybs, BF16, [2, T])
            wp = view(wpb, BF16, [4, 2, 256])
            dma("pool", wp, w_pool.rearrange("g (c p) d -> p g c d", p=128), (), [(wpb, 0)])
            pool_ov = pool_out
            for g in range(4):
                win = 2 ** (g + 1)
                slot = load_w16(wv[:, :, 2048 + g * 256:2048 + (g + 1) * 256], 256)
                memset("pool", X[:, :, 0:1], 0.0, [(xbb, "h")])
                dma("sp", X[:, :, SEG[1][0]:SEG[1][0] + 16], phist_in[:, 2 * g:2 * g + 2, 0, :], (), [(xbb, "h1")])
                dma("sp", X[:, :, SEG[2][0]:SEG[2][0] + 16], phist_in[:, 2 * g:2 * g + 2, 1, :], (), [(xbb, "h2")])
                for cc in range(2):
                    bank = next_bank()
                    for k in range(16):
                        mm(psum[:, bank, 0:16], W_.slots[slot][:, k, cc * 128:(cc + 1) * 128], hh[:, k, :], k == 0, k == 15,
                           [(W_.buf, slot), (hhb, 5)], [PS(bank)])
                    ts("dve", X[:, cc, 0:16], psum[:, bank, 0:16], cflag_s[:, 0:1], None, ALU.mult, None, [PS(bank), CK], [(xbb, "h")])
                    for ti, (c0, w) in enumerate(TILES):
                        bank = next_bank()
                        for k in range(16):
                            mm(psum[:, bank, 0:w], W_.slots[slot][:, k, cc * 128:(cc + 1) * 128], hT[:, k, c0:c0 + w], k == 0, k == 15,
                               [(W_.buf, slot), HK[ti]], [PS(bank)])
                        if c0 < TP:
                            copy("act", X[:, cc, 16 + c0:16 + c0 + w], psum[:, bank, 0:w], [PS(bank)], [(xbb, ti)])
                        else:
                            copy("act", X[:, cc, SEG[1][0] + 16:SEG[1][0] + 48], psum[:, bank, 0:32], [PS(bank)], [(xbb, ti)])
                            copy("act", X[:, cc, SEG[2][0] + 16:SEG[2][0] + 48], psum[:, bank, 32:64], [PS(bank)], [(xbb, ti)])
                XK = [(xbb, "h"), (xbb, "h1"), (xbb, "h2")] + [(xbb, i) for i in range(5)]
                for si, (s0, sl) in enumerate(SEG):
                    dma("sp", pool_ov[:, 2 * g:2 * g + 2, si, :], X[:, :, s0 + sl - 16:s0 + sl], XK, [("pool_o", g, si)])
                src, srck = X, XK
                bufs = [(P1, [(p1b, 0)]), (P2, [(p2b, 0)])]
                for lev in range(g + 1):
                    sh = 2 ** lev
                    dst, dstk = bufs[lev % 2]
                    for (s0, sl) in SEG:
                        tt("pool", dst[:, :, s0 + sh:s0 + sl], src[:, :, s0 + sh:s0 + sl], src[:, :, s0:s0 + sl - sh], ALU.add, srck, dstk)
                    src, srck = dst, dstk
                for si, (s0, sl) in enumerate(SEG):
                    tc0 = 0 if si == 0 else (TP + 32 * (si - 1))
                    n = sl - 16
                    stt("dve", pooled[:, :, tc0:tc0 + n], src[:, :, s0 + 16:s0 + sl], 1.0 / win, X[:, :, s0 + 16:s0 + sl], ALU.mult, ALU.subtract,
                        srck + XK, [(pob, si)])
                tmp16 = P1[:, :, 0:16] if (g % 2 == 1) else P2[:, :, 0:16]
                tmpk = [(p1b, 0)] if (g % 2 == 1) else [(p2b, 0)]
                tt("pool", tmp16, src[:, :, 16:32], icnt_s[:, g, :].unsqueeze(1).to_broadcast([128, 2, 16]), ALU.mult,
                   srck + [CK], tmpk)
                tt("pool", pooled[:, :, 0:16], tmp16, X[:, :, 16:32], ALU.subtract, tmpk + XK, [(pob, 0)])
                PK = [(pob, i) for i in range(3)]
                for dd in range(2):
                    for ti, (c0, w) in enumerate(TILES):
                        bank = next_bank()
                        for cc in range(2):
                            mm(psum[:, bank, 0:w], wp[:, g, cc, dd * 128:(dd + 1) * 128], pooled[:, cc, c0:c0 + w], cc == 0, cc == 1,
                               [(wpb, 0)] + PK, [PS(bank)])
                        ts("dve", ybst[:, dd, c0:c0 + w], psum[:, bank, 0:w], pscale_s[:, 2 * g + dd:2 * g + dd + 1], None, ALU.mult, None,
                           [PS(bank), CK], [(ybs, 0)])
                dma("sp", ybT_d[g * 256:(g + 1) * 256, :].rearrange("(c p) t -> p c t", p=128), ybst, [(ybs, 0)], [("ybT", g)])
            for b in (xbb, p1b, p2b, pob, ybs, wpb, hhb):
                P.free(b)

            ubuf = P.alloc("uT", 8 * T * 2)
            uT = view(ubuf, BF16, [8, T])
            for blk in range(2):
                slot = load_w16(wv[:, :, blk * 512:(blk + 1) * 512], 512)
                for n in range(4):
                    for ti, (c0, w) in enumerate(TILES):
                        bank = next_bank()
                        for k in range(16):
                            mm(psum[:, bank, 0:w], W_.slots[slot][:, k, n * 128:(n + 1) * 128], hT[:, k, c0:c0 + w], k == 0, k == 15,
                               [(W_.buf, slot), HK[ti]], [PS(bank)])
                        act(uT[:, blk * 4 + n, c0:c0 + w], psum[:, bank, 0:w], AF.Gelu_apprx_tanh, [PS(bank)], [(ubuf, blk * 4 + n, ti)])

            l0c = P.alloc("l0c", 2 * 4096 + 2048 + 1024 + 4096 + 2048)
            base4 = l0c.start // 4
            lng = arena[:, base4:base4 + 1024]
            lnb = arena[:, base4 + 1024:base4 + 2048]
            wsT = arena[:, base4 + 2048:base4 + 2048 + 512].bitcast(BF16).rearrange("p (g t) -> p g t", g=8)
            wsTs = arena[:, base4 + 2560:base4 + 2560 + 256].bitcast(BF16).rearrange("p (g t) -> p g t", g=8)
            bsbc = arena[:, base4 + 2816:base4 + 2816 + 1024].rearrange("p (g t) -> p g t", g=8)
            bsbcs = arena[:, base4 + 3840:base4 + 3840 + 512].rearrange("p (g t) -> p g t", g=8)
            L0K = (l0c, 0)
            wstmp = P.alloc("wstmp", 4096 + 2048)
            wst_f = view(wstmp, F32, [8, 128])
            wsts_f = arena[:, (wstmp.start + 4096) // 4:(wstmp.start + 4096) // 4 + 512].rearrange("p (g t) -> p g t", g=8)
            dma("sp", lng, lng_in, (), [L0K])
            dma("sp", lnb, lnb_in, (), [L0K])
            dma("sp", bsbc, bsbc_in, (), [L0K])
            dma("sp", bsbcs, bsbcs_in, (), [L0K])
            dma("sp", wst_f, wsT_in, (), [(wstmp, 0)])
            dma("sp", wsts_f[0:64], wsTs_in, (), [(wstmp, 1)])
            memset("pool", wst_f[64:128, :, 0:64], 0.0, [(wstmp, 0)])
            copy("pool", wsT, wst_f, [(wstmp, 0)], [L0K])
            copy("pool", wsTs[0:64], wsts_f[0:64], [(wstmp, 1)], [L0K])
            s0 = load_w16(wv[:, :, 1024:1536], 512)
            s1 = load_w16(wv[:, :, 1536:2048], 512)
            vbuf = [P.alloc("vg0", 4096), P.alloc("vg1", 4096)]
            vnbuf = [P.alloc("vn0", 4096), P.alloc("vn1", 4096)]
            vbb = [P.alloc("vb0", 2048), P.alloc("vb1", 2048)]
            stb = P.alloc("vstat", 256)
            gtb = [P.alloc("gt0", 4 * 128 * 4), P.alloc("gt1", 4 * 128 * 4)]
            stv = view(stb, F32, [64])
            NB = 17

            def vparams(bi):
                c0 = bi * 128
                m = 128 if bi < 16 else 64
                return (c0, m, view(vbuf[bi % 2], F32, [1024])[0:m], view(vnbuf[bi % 2], F32, [1024])[0:m], view(vbb[bi % 2], BF16, [1024])[0:m],
                        (vbuf[bi % 2], 0), (vnbuf[bi % 2], 0), (vbb[bi % 2], 0), min(bi // 4, 4))

            def pstage(bi):
                c0 = bi * 128
                m = 128 if bi < 16 else 64
                vg = view(vbuf[bi % 2], F32, [1024])[0:m]
                vn = view(vnbuf[bi % 2], F32, [1024])[0:m]
                vb = view(vbb[bi % 2], BF16, [1024])[0:m]
                VGK, VNK, VBK = (vbuf[bi % 2], 0), (vnbuf[bi % 2], 0), (vbb[bi % 2], 0)
                sto = (bi % 2) * 32
                st = stv[0:m, sto:sto + 32]
                STK = (stb, bi % 2)
                ti = min(bi // 4, 4)
                for half, slot in ((0, s0), (1, s1)):
                    bank = next_bank()
                    for k in range(16):
                        mm(psum[0:m, bank, :], hT[:, k, c0:c0 + m], W_.slots[slot][:, k, :], k == 0, k == 15,
                           [(W_.buf, slot), HK[ti]], [PS(bank)])
                    act(vg[:, half * 512:(half + 1) * 512], psum[0:m, bank, :], AF.Gelu_apprx_tanh, [PS(bank)], [VGK])
                for half in range(2):
                    P.emit("dve", lambda e, o=st[:, half * 6:half * 6 + 6], i=vg[:, half * 512:(half + 1) * 512]: e.bn_stats(out=o, in_=i), [VGK], [STK])
                P.emit("dve", lambda e, o=st[:, 12:14], i=st[:, 0:12].rearrange("p (a b) -> p a b", a=2): e.bn_aggr(out=o, in_=i), [STK], [STK])
                act(st[:, 14:15], st[:, 13:14], AF.Sqrt, [STK], [STK], bias=eps_ap[0:m], scale=1.0)
                P.emit("dve", lambda e, o=st[:, 15:16], i=st[:, 14:15]: e.reciprocal(out=o, in_=i), [STK], [STK])
                ts("dve", vn, vg, st[:, 12:13], st[:, 15:16], ALU.subtract, ALU.mult, [VGK, STK], [VNK])
                tt("pool", vn, vn, lng[0:m], ALU.mult, [VNK, L0K], [VNK])
                if bi < 16:
                    tt("pool", vb, vn, lnb[0:m], ALU.add, [VNK, L0K], [VBK])
                else:
                    tt("pool", vn, vn, lnb[0:m], ALU.add, [VNK, L0K], [VNK])
                    copy("pool", vb, vn, [VNK], [VBK])
                    dma("sp", vs_out, vn, [VNK], [("vs_o",)])
            def qstage(bi):
                c0, m, vg, vn, vb, VGK, VNK, VBK, ti = vparams(bi)
                for gh in range(2):
                    bank = next_bank()
                    pv = psum[:, bank, :].rearrange("p (g t) -> p g t", g=4)
                    for gi in range(4):
                        g = gh * 4 + gi
                        rhs = wsT[:, g, :] if bi < 16 else wsTs[0:64, g, :]
                        mm(pv[:, gi, 0:m], vb[:, g * 128:(g + 1) * 128], rhs, True, True, [VBK, L0K], [PS(bank)])
                    gt = view(gtb[gh], F32, [4, 128])
                    bs_ = bsbc[:, gh * 4:gh * 4 + 4, :] if bi < 16 else bsbcs[:, gh * 4:gh * 4 + 4, :]
                    tt("dve", gt[:, :, 0:m], pv[:, :, 0:m], bs_, ALU.add, [PS(bank), L0K], [(gtb[gh], 0)])
                    ukeys = [(ubuf, gh * 4 + gi, ti) for gi in range(4)]
                    tt("dve", uT[:, gh * 4:gh * 4 + 4, c0:c0 + m], gt[:, :, 0:m], uT[:, gh * 4:gh * 4 + 4, c0:c0 + m], ALU.mult,
                       [(gtb[gh], 0)] + ukeys, ukeys)
            for bi in range(NB + 1):
                if bi < NB:
                    pstage(bi)
                if bi >= 1:
                    qstage(bi - 1)
            for b in vbuf + vnbuf + vbb + [stb, wstmp, l0c] + gtb:
                P.free(b)
            P.free(hbuf)

            ybuf = P.alloc("ybT", 8 * T * 2)
            ybT = view(ybuf, BF16, [8, T])
            dma("sp", ybT, ybT_d.rearrange("(c p) t -> p c t", p=128), [("ybT", g) for g in range(4)], [(ybuf, 0)])
            xrb = P.alloc("xr", 4 * 512 * 4)
            wo = w_out_ab.rearrange("(c p) n -> p c n", p=128)
            nxt = load_w16(wo[:, :, 0:512], 512)
            cnt = 0
            for blk in range(4):
                cur = nxt
                if blk < 3:
                    nxt = load_w16(wo[:, :, (blk + 1) * 512:(blk + 2) * 512], 512)
                for n4 in range(4):
                    n = blk * 4 + n4
                    for ti, (c0, w) in enumerate(TILES):
                        bank = next_bank()
                        for k in range(16):
                            src = uT[:, k, c0:c0 + w] if k < 8 else ybT[:, k - 8, c0:c0 + w]
                            rk = [(ubuf, k, ti)] if k < 8 else [(ybuf, 0)]
                            mm(psum[:, bank, 0:w], W_.slots[cur][:, k, n4 * 128:(n4 + 1) * 128], src, k == 0, k == 15, [(W_.buf, cur)] + rk, [PS(bank)])
                        resid_update(xT_in, xs, n, c0, w, bank, mod_s[0][:, 2], 0, xrb, cnt % 4)
                        cnt += 1
            for b in (ybuf, xrb, ubuf):
                P.free(b)

        def ffn(l):
            hbuf = P.alloc("hT", 16 * T * 2)
            hT = view(hbuf, BF16, [16, T])
            norm_phase(xs, hT, hbuf, Affn[l], mod_s[l][:, 3], l)
            HK = [(hbuf, i) for i in range(5)]
            wg = w_gu[l].rearrange("(c p) n -> p c n", p=128)
            gub = P.alloc("guw", 2 * 16 * 1024 * 2)
            guw = view(gub, BF16, [2, 16, 1024])
            astb = [P.alloc("ast0", T * 2), P.alloc("ast1", T * 2)]
            sgb = [P.alloc("sg0", 2048), P.alloc("sg1", 2048)]

            def load_gu(fb):
                i = fb % 2
                dma("pool", guw[:, i, :, 0:512], wg[:, :, fb * 512:(fb + 1) * 512], (), [(gub, i)])
                dma("pool", guw[:, i, :, 512:1024], wg[:, :, DFF + fb * 512:DFF + (fb + 1) * 512], (), [(gub, i)])
            load_gu(0)
            k_ = 0
            for fb in range(11):
                if fb + 1 < 11:
                    load_gu(fb + 1)
                i = fb % 2
                for f4 in range(4):
                    f = fb * 4 + f4
                    ast = view(astb[f % 2], BF16, [T])
                    for ti, (c0, w) in enumerate(TILES):
                        bg, bu = next_bank(), next_bank()
                        for k in range(16):
                            mm(psum[:, bg, 0:w], guw[:, i, k, f4 * 128:(f4 + 1) * 128], hT[:, k, c0:c0 + w], k == 0, k == 15, [(gub, i), HK[ti]], [PS(bg)])
                        for k in range(16):
                            mm(psum[:, bu, 0:w], guw[:, i, k, 512 + f4 * 128:512 + (f4 + 1) * 128], hT[:, k, c0:c0 + w], k == 0, k == 15, [(gub, i), HK[ti]], [PS(bu)])
                        sg = view(sgb[k_ % 2], F32, [512])[:, 0:w]
                        act(sg, psum[:, bg, 0:w], AF.Silu, [PS(bg)], [(sgb[k_ % 2], 0)])
                        tt("dve", ast[:, c0:c0 + w], psum[:, bu, 0:w], sg, ALU.mult, [PS(bu), (sgb[k_ % 2], 0)], [(astb[f % 2], ti)])
                        k_ += 1
                    dma("sp", aT_d[f * 128:(f + 1) * 128, :], ast, [(astb[f % 2], ti) for ti in range(5)], [("aT", f)])
            for b in [gub] + astb + sgb + [hbuf]:
                P.free(b)
            ws_close()
            wd = w_down[l].rearrange("(c p) n -> p c n", p=128)
            wdb = [P.alloc("wd0", 44 * 512 * 2), P.alloc("wd1", 44 * 512 * 2)]
            atb = [P.alloc("at0", 44 * 512 * 2), P.alloc("at1", 44 * 512 * 2)]
            xrb = P.alloc("xr", 4 * 512 * 4)
            aTv = aT_d.rearrange("(c p) t -> p c t", p=128)
            AK = [("aT", f) for f in range(44)]

            def load_wd(nb):
                wt = view(wdb[nb % 2], BF16, [44, 512])
                for q in range(4):
                    dma("pool", wt[:, q * 11:(q + 1) * 11, :], wd[:, q * 11:(q + 1) * 11, nb * 512:(nb + 1) * 512], (), [(wdb[nb % 2], q)])

            def load_at(j):
                ti = j % 5
                c0, w = TILES[ti]
                at = view(atb[j % 2], BF16, [44, 512])
                for q in range(2):
                    dma("act", at[:, q * 22:(q + 1) * 22, 0:w], aTv[:, q * 22:(q + 1) * 22, c0:c0 + w], AK, [(atb[j % 2], q)])
            load_wd(0)
            load_at(0)
            j = 0
            cnt = 0
            for nb in range(4):
                if nb + 1 < 4:
                    load_wd(nb + 1)
                wt = view(wdb[nb % 2], BF16, [44, 512])
                for ti, (c0, w) in enumerate(TILES):
                    if j + 1 < 20:
                        load_at(j + 1)
                    at = view(atb[j % 2], BF16, [44, 512])
                    for n4 in range(4):
                        n = nb * 4 + n4
                        bank = next_bank()
                        for f in range(44):
                            mm(psum[:, bank, 0:w], wt[:, f, n4 * 128:(n4 + 1) * 128], at[:, f, 0:w], f == 0, f == 43,
                               [(wdb[nb % 2], f // 11), (atb[j % 2], f // 22)], [PS(bank)])
                        resid_update(xs, xs, n, c0, w, bank, mod_s[l][:, 5], l, xrb, cnt % 4)
                        cnt += 1
                    j += 1
            for b in wdb + atb + [xrb]:
                P.free(b)
            ws_open()


        GAM = [1.0 - 2.0 ** (-5 - h) for h in range(8)]

        def transpose(out, in_, ident, reads, writes):
            return P.emit("pe", lambda e, o=out, i=in_, d=ident: e.transpose(o, i, d), reads, writes)

        def layer1_mixer():
            l = 1
            hbuf = P.alloc("hT", 16 * T * 2)
            hT = view(hbuf, BF16, [16, T])
            norm_phase(xs, hT, hbuf, Amix[1], mod_s[1][:, 0], 1)
            HK = [(hbuf, i) for i in range(5)]
            wv = w_in_c.rearrange("(c p) n -> p c n", p=128)
            tabb = P.alloc("l1tab", 2 * T * 4 + 32 * 4 + 128 * 4)
            cosT = view(tabb, F32, [2, T])[:, 0]
            sinT = view(tabb, F32, [2, T])[:, 1]
            dtab = arena[:, (tabb.start + 2 * T * 4) // 4:(tabb.start + 2 * T * 4) // 4 + 32]
            dtab2 = arena[:, (tabb.start + 2 * T * 4) // 4 + 32:(tabb.start + 2 * T * 4) // 4 + 160]
            TK = (tabb, 0)
            dma("sp", cosT, cos_in, (), [TK])
            dma("sp", sinT, sin_in, (), [TK])
            dma("sp", dtab, dtab_in, (), [TK])
            dma("sp", dtab2, dtab2_in, (), [TK])
            kd2b = P.alloc("kds2", 16 * 256 * 2)
            kd2v = kd2_d.rearrange("(b p) n -> p b n", p=128)
            stg = [P.alloc("qkst0", 2 * T * 2), P.alloc("qkst1", 2 * T * 2)]
            rtb = [P.alloc("rt0", 4 * 512 * 4), P.alloc("rt1", 4 * 512 * 4)]
            kdsb = [P.alloc("kds0", 17 * 256 * 2), P.alloc("kds1", 17 * 256 * 2)]
            kdv = kd_d.rearrange("(b p) n -> p b n", p=128)
            hcount = 0
            rc = 0
            for blk in range(8):
                slot = load_w16(wv[:, :, blk * 512:(blk + 1) * 512], 512)
                isk = blk >= 4
                ks = 0.0625 if isk else 1.0
                for hh in range(2):
                    head = (blk % 4) * 2 + hh
                    sb_ = stg[hcount % 2]
                    st_ = view(sb_, BF16, [2, T])
                    for ti, (c0, w) in enumerate(TILES):
                        b1, b2 = next_bank(), next_bank()
                        for half, bank in ((0, b1), (1, b2)):
                            n = hh * 2 + half
                            for k in range(16):
                                mm(psum[:, bank, 0:w], W_.slots[slot][:, k, n * 128:(n + 1) * 128], hT[:, k, c0:c0 + w], k == 0, k == 15,
                                   [(W_.buf, slot), HK[ti]], [PS(bank)])
                        rb = rtb[rc % 2]
                        rt = view(rb, F32, [4, 512])
                        rc += 1
                        cs, sn = cosT[:, c0:c0 + w], sinT[:, c0:c0 + w]
                        stt("dve", rt[:, 0, 0:w], psum[:, b1, 0:w], ks, cs, ALU.mult, ALU.mult, [PS(b1), TK], [(rb, 0)])
                        stt("dve", rt[:, 1, 0:w], psum[:, b2, 0:w], ks, sn, ALU.mult, ALU.mult, [PS(b2), TK], [(rb, 1)])
                        stt("dve", rt[:, 2, 0:w], psum[:, b1, 0:w], ks, sn, ALU.mult, ALU.mult, [PS(b1), TK], [(rb, 2)])
                        stt("dve", rt[:, 3, 0:w], psum[:, b2, 0:w], ks, cs, ALU.mult, ALU.mult, [PS(b2), TK], [(rb, 3)])
                        tt("pool", st_[:, 0, c0:c0 + w], rt[:, 0, 0:w], rt[:, 1, 0:w], ALU.subtract, [(rb, 0), (rb, 1)], [(sb_, ti)])
                        tt("pool", st_[:, 1, c0:c0 + w], rt[:, 2, 0:w], rt[:, 3, 0:w], ALU.add, [(rb, 2), (rb, 3)], [(sb_, ti)])
                    SK = [(sb_, ti) for ti in range(5)]
                    dst = kT_d if isk else qT_d
                    dma("sp", dst[head * 256:(head + 1) * 256, :].rearrange("(c p) t -> p c t", p=128), st_, SK, [("qk", isk, head)])
                    if isk:
                        kb_ = kdsb[head % 2]
                        kds = view(kb_, BF16, [17, 256])
                        for bi in range(17):
                            c0 = bi * 128
                            m = 128 if bi < 16 else 64
                            bank = next_bank()
                            pT = psum[:, bank, :].bitcast(BF16)[:, 0:256].rearrange("p (c d) -> p c d", c=2)
                            for dc in range(2):
                                transpose(pT[0:m, dc, :], st_[:, dc, c0:c0 + m], identb, SK + [("identb",)], [PS(bank)])
                            dcol = 8 + head if bi < 16 else 16 + head
                            ts("dve", kds[0:m, bi, :], pT[0:m].rearrange("p c d -> p (c d)"), dtab[0:m, dcol:dcol + 1], None, ALU.mult, None,
                               [PS(bank), TK], [(kb_, bi)])
                            if bi < 16:
                                ts("dve", view(kd2b, BF16, [16, 256])[:, bi, :], pT.rearrange("p c d -> p (c d)"), dtab2[:, bi * 8 + head:bi * 8 + head + 1], None,
                                   ALU.mult, None, [PS(bank), TK], [(kd2b, bi)])
                        dma("sp", kdv[:, 0:16, head * 256:(head + 1) * 256], kds[:, 0:16, :], [(kb_, bi) for bi in range(16)], [("kd", head, 0)])
                        dma("sp", kdv[0:64, 16, head * 256:(head + 1) * 256], kds[0:64, 16, :], [(kb_, 16)], [("kd", head, 1)])
                        dma("sp", kd2v[:, :, head * 256:(head + 1) * 256], view(kd2b, BF16, [16, 256]), [(kd2b, bi) for bi in range(16)], [("kd2", head)])
                    hcount += 1
            for b in stg + rtb + kdsb + [tabb, kd2b]:
                P.free(b)
            vst = [P.alloc("vst0", 17 * 512 * 2), P.alloc("vst1", 17 * 512 * 2)]
            vdv = v_d.rearrange("(b p) n -> p b n", p=128)
            sgv = sg_d.rearrange("(b p) n -> p b n", p=128)

            def a2_block(cb):
                isg = cb >= 8
                head = cb % 8
                col0 = (8192 if isg else 4096) + head * 512
                slot = load_w16(wv[:, :, col0:col0 + 512], 512)
                vb_ = vst[cb % 2]
                vs_ = view(vb_, BF16, [17, 512])
                for bi in range(17):
                    c0 = bi * 128
                    m = 128 if bi < 16 else 64
                    ti = min(bi // 4, 4)
                    bank = next_bank()
                    for k in range(16):
                        mm(psum[0:m, bank, :], hT[:, k, c0:c0 + m], W_.slots[slot][:, k, :], k == 0, k == 15, [(W_.buf, slot), HK[ti]], [PS(bank)])
                    if isg:
                        act(vs_[0:m, bi, :], psum[0:m, bank, :], AF.Silu, [PS(bank)], [(vb_, bi)])
                    else:
                        copy("dve", vs_[0:m, bi, :], psum[0:m, bank, :], [PS(bank)], [(vb_, bi)])
                dstv = sgv if isg else vdv
                dma("sp", dstv[:, 0:16, head * 512:(head + 1) * 512], vs_[:, 0:16, :], [(vb_, bi) for bi in range(16)], [("vg", isg, head, 0)])
                dma("sp", dstv[0:64, 16, head * 512:(head + 1) * 512], vs_[0:64, 16, :], [(vb_, 16)], [("vg", isg, head, 1)])
            for cb in range(8):
                a2_block(cb)

            pk = [P.alloc("pk0", 16 * 256 * 2)] * 2
            pv = [P.alloc("pv0", 16 * 512 * 2)] * 2
            slb = [P.alloc("sl0", 2 * 512 * 4), P.alloc("sl1", 2 * 512 * 4)]
            for h in range(8):
                kk = view(pk[h % 2], BF16, [16, 256])
                vv = view(pv[h % 2], BF16, [16, 512])
                Sl = view(slb[h % 2], F32, [2, 512])
                dma("sp", kk, kd2v[:, :, h * 256:(h + 1) * 256], [("kd2", h)], [(pk[h % 2], 0)])
                dma("sp", vv, vdv[:, 0:16, h * 512:(h + 1) * 512], [("vg", False, h, 0)], [(pv[h % 2], 0)])
                for dc in range(2):
                    bank = next_bank()
                    for bi in range(16):
                        mm(psum[:, bank, :], kk[:, bi, dc * 128:(dc + 1) * 128], vv[:, bi, :], bi == 0, bi == 15, [(pk[h % 2], 0), (pv[h % 2], 0)], [PS(bank)])
                    copy("dve", Sl[:, dc, :], psum[:, bank, :], [PS(bank)], [(slb[h % 2], dc)])
                dma("sp", sloc_d[h].ap().rearrange("(c p) e -> p c e", p=128), Sl, [(slb[h % 2], 0), (slb[h % 2], 1)], [("sloc", h)])
                P.emit("pool", lambda e, i=sloc_d[h], o=sall_d[h]: e.collective_compute("AllGather", ALU.bypass, replica_groups=[list(range(8))],
                                                                                 ins=[i.ap().opt()], outs=[o.ap().opt()]),
                       [("sloc", h)], [("sall", h)], dma=True, cc=True)
            for b in pk[:1] + pv[:1] + slb:
                P.free(b)
            for cb in range(8, 16):
                a2_block(cb)
            for b in vst + [hbuf]:
                P.free(b)
            ws_close()

            ctb = P.alloc("l1c", 8 * 128 * 4 + 64 * 4 + 32 * 4)
            decT = view(ctb, F32, [8, 128])
            coef = arena[:, (ctb.start + 4096) // 4:(ctb.start + 4096) // 4 + 64]
            dtab = arena[:, (ctb.start + 4096 + 256) // 4:(ctb.start + 4096 + 256) // 4 + 32]
            CTK = (ctb, 0)
            dma("sp", decT, decT_in, (), [CTK])
            dma("sp", coef, coef_in, (), [CTK])
            dma("sp", dtab, dtab_in, (), [CTK])
            sab2 = [P.alloc("sall0", 2 * 512 * 4), P.alloc("sall1", 2 * 512 * 4)]

            qb = [P.alloc("bq0", 2 * T * 2), P.alloc("bq1", 2 * T * 2)]
            kb = [P.alloc("bk0", 2 * T * 2), P.alloc("bk1", 2 * T * 2)]
            kdb = [P.alloc("bkd0", 18 * 256 * 2), P.alloc("bkd1", 18 * 256 * 2)]
            vb2 = [P.alloc("bv0", 18 * 512 * 2), P.alloc("bv1", 18 * 512 * 2)]
            sgb2 = [P.alloc("bsg0", 18 * 512 * 2)] * 2
            ogb = P.alloc("ogst", 4 * T * 2)
            ogst = view(ogb, BF16, [4, T])
            Sb2 = [P.alloc("Sst0", 3 * 2 * 512 * 4), P.alloc("Sst1", 3 * 2 * 512 * 4)]
            Sbf2 = [P.alloc("Sbf0", 3 * 2 * 512 * 2), P.alloc("Sbf1", 3 * 2 * 512 * 2)]
            tmpb = [P.alloc("rtm0", 2 * 512 * 4 + 512 * 2 + 128 * 2 + 64), P.alloc("rtm1", 2 * 512 * 4 + 512 * 2 + 128 * 2 + 64)]
            qTv = qT_d.rearrange("(h c p) t -> p h c t", h=8, p=128)
            kTv = kT_d.rearrange("(h c p) t -> p h c t", h=8, p=128)
            cnt = 0
            for h in range(8):
                i = h % 2
                qh = view(qb[i], BF16, [2, T])
                kh = view(kb[i], BF16, [2, T])
                kdh = view(kdb[i], BF16, [18, 256])
                vh = view(vb2[i], BF16, [18, 512])
                sgh = view(sgb2[i], BF16, [18, 512])
                dma("sp", qh, qTv[:, h], [("qk", False, h)], [(qb[i], 0)])
                dma("sp", kh, kTv[:, h], [("qk", True, h)], [(kb[i], 0)])
                dma("sp", kdh[:, 0:16, :], kdv[:, 0:16, h * 256:(h + 1) * 256], [("kd", h, 0)], [(kdb[i], 0)])
                dma("sp", vh[:, 0:16, :], vdv[:, 0:16, h * 512:(h + 1) * 512], [("vg", False, h, 0)], [(vb2[i], 0)])
                dma("sp", sgh[:, 0:16, :], sgv[:, 0:16, h * 512:(h + 1) * 512], [("vg", True, h, 0)], [(sgb2[i], 0)])
                for (dst_, src_, key, bk) in ((kdh, kd_d, ("kd", h, 1), kdb[i]), (vh, v_d, ("vg", False, h, 1), vb2[i]), (sgh, sg_d, ("vg", True, h, 1), sgb2[i])):
                    wd_ = 256 if src_ is kd_d else 512
                    dma("sp", dst_[0:32, 16:18, :], src_[TP:TP + 64, h * wd_:(h + 1) * wd_].rearrange("(s p) n -> p s n", p=32), [key], [(bk, 1)])
                INK = [(qb[i], 0), (kb[i], 0), (kdb[i], 0), (kdb[i], 1), (vb2[i], 0), (vb2[i], 1), (sgb2[i], 0), (sgb2[i], 1)]
                SS = [view(b_, F32, [3, 2, 512]) for b_ in Sb2]
                SBF = [view(b_, BF16, [3, 2, 512]) for b_ in Sbf2]
                sav = sall_d[h].ap().rearrange("(r c p) e -> p r c e", r=8, p=128)
                for r in range(8):
                    sb_ = sab2[r % 2]
                    sa = view(sb_, F32, [2, 512])
                    dma("sp", sa, sav[:, r], [("sall", h)], [(sb_, 0)])
                    cf = coef[:, r * 8 + h:r * 8 + h + 1]
                    if r == 0:
                        ts("dve", SS[0][:, 0], sa, cf, None, ALU.mult, None, [(sb_, 0), CTK], [(Sb2[0], 0)])
                    else:
                        stt("dve", SS[0][:, 0], sa, cf, SS[0][:, 0], ALU.mult, ALU.add, [(sb_, 0), CTK, (Sb2[0], 0)], [(Sb2[0], 0)])
                copy("act", SBF[0][:, 0], SS[0][:, 0], [(Sb2[0], 0)], [(Sbf2[0], 0)])
                for sq, par in ((0, 0), (1, 1)):
                    dma("sp", SS[par][:, 1 + sq], sret_in[sq, h].rearrange("(c p) e -> p c e", p=128), (), [(Sb2[par], 1 + sq)])
                    copy("act", SBF[par][:, 1 + sq], SS[par][:, 1 + sq], [(Sb2[par], 1 + sq)], [(Sbf2[par], 1 + sq)])
                ctx = {}

                def cparams(ci):
                    if ci < 16:
                        return 128, ci * 128, 0, GAM[h] ** 128
                    return 32, TP + 32 * (ci - 16), 1 + (ci - 16), GAM[h] ** 32

                def tviews(tb):
                    b4 = tb.start // 4
                    return (arena[:, b4:b4 + 512], arena[:, b4 + 512:b4 + 1024], arena[:, b4 + 1024:b4 + 1280].bitcast(BF16),
                            arena[:, b4 + 1280:b4 + 1344].bitcast(BF16), arena[:, b4 + 1344:b4 + 1360])

                def stageA(ci, h=h, qh=qh, kh=kh, kdh=kdh, vh=vh, INK=INK):
                    m, c0, ss_, gL = cparams(ci)
                    par, npar = ci % 2, (ci + 1) % 2
                    tb = tmpb[ci % 2]
                    tf, of, ogt, scT, stat = tviews(tb)
                    bB, bC = (0, 1) if ci % 2 == 0 else (2, 3)
                    bA, bE0, bE1 = 4, 5, 6
                    ctx[ci] = (bB, bC, tb)
                    for dc, bE in ((0, bE0), (1, bE1)):
                        mm(psum[:, bE, :], kdh[0:m, ci, dc * 128:(dc + 1) * 128], vh[0:m, ci, :], True, True, INK, [PS(bE)])
                    for dc in range(2):
                        mm(psum[0:m, bA, 0:m], kh[:, dc, c0:c0 + m], qh[:, dc, c0:c0 + m], dc == 0, dc == 1, INK, [PS(bA)])
                    for dc, bE in ((0, bE0), (1, bE1)):
                        stt("dve", SS[npar][:, ss_, dc, :], SS[par][:, ss_, dc, :], gL, psum[:, bE, :], ALU.mult, ALU.add,
                            [PS(bE), (Sb2[par], ss_)], [(Sb2[npar], ss_)])
                    tt("dve", scT[0:m, 0:m], psum[0:m, bA, 0:m], decT[0:m, h, 0:m], ALU.mult, [PS(bA), CTK], [(tb, "sc")])
                    if ci < 15:
                        copy("act", SBF[npar][:, ss_], SS[npar][:, ss_], [(Sb2[npar], ss_)], [(Sbf2[npar], ss_)])
                    elif ci == 15:
                        dma("sp", retp_out[h].rearrange("(c p) e -> p c e", p=128), SS[npar][:, 0], [(Sb2[npar], 0)], [("retp", h)])
                    else:
                        dma("sp", rets_out[ci - 16, h].rearrange("(c p) e -> p c e", p=128), SS[npar][:, ss_], [(Sb2[npar], ss_)], [("rets", h, ci)])
                    mm(psum[0:m, bB, :], scT[0:m, 0:m], vh[0:m, ci, :], True, True, [(tb, "sc")] + INK, [PS(bB)])
                    for dc in range(2):
                        mm(psum[0:m, bC, :], qh[:, dc, c0:c0 + m], SBF[par][:, ss_, dc, :], dc == 0, dc == 1, INK + [(Sbf2[par], ss_)], [PS(bC)])

                def stageB(ci, h=h, sgh=sgh, INK=INK):
                    m, c0, ss_, gL = cparams(ci)
                    bB, bC, tb = ctx[ci]
                    tf, of, ogt, scT, stat = tviews(tb)
                    bD = 7
                    copy("act", tf[0:m], psum[0:m, bB, :], [PS(bB)], [(tb, "t")])
                    stt("dve", of[0:m], psum[0:m, bC, :], dtab[0:m, h:h + 1], tf[0:m], ALU.mult, ALU.add, [PS(bC), (tb, "t"), CTK], [(tb, "o")])
                    act(tf[0:m], of[0:m], AF.Square, [(tb, "o"), (tb, "t")], [(tb, "t"), (tb, "st")], accum_out=stat[0:m, 0:1])
                    act(stat[0:m, 1:2], stat[0:m, 0:1], AF.Sqrt, [(tb, "st")], [(tb, "st")], bias=eps_ap[0:m], scale=1.0 / 512)
                    P.emit("dve", lambda e, o=stat[0:m, 2:3], i_=stat[0:m, 1:2]: e.reciprocal(out=o, in_=i_), [(tb, "st")], [(tb, "st")])
                    stt("dve", ogt[0:m], of[0:m], stat[0:m, 2:3], sgh[0:m, ci, :], ALU.mult, ALU.mult, [(tb, "o"), (tb, "st")] + INK, [(tb, "og")])
                    pT = psum[:, bD, :].bitcast(BF16)[:, 0:512].rearrange("p (c l) -> p c l", c=4)
                    for e4 in range(4):
                        transpose(pT[:, e4, 0:m], ogt[0:m, e4 * 128:(e4 + 1) * 128], identb[0:m, 0:m], [(tb, "og"), ("identb",)], [PS(bD)])
                    copy("act", ogst[:, :, c0:c0 + m], pT[:, :, 0:m], [PS(bD)], [(ogb, ci)])
                for ci in range(19):
                    if ci < 18:
                        stageA(ci)
                    if ci >= 1:
                        stageB(ci - 1)
                dma("sp", ogT_d[h * 512:(h + 1) * 512, :].rearrange("(c p) t -> p c t", p=128), ogst, [(ogb, ci) for ci in range(18)], [("ogT", h)])
            for b in qb + kb + kdb + vb2 + sgb2[:1] + [ogb, ctb] + Sb2 + Sbf2 + sab2 + tmpb:
                P.free(b)

            ob = P.alloc("ogT", 32 * T * 2)
            ogT = view(ob, BF16, [32, T])
            for q4 in range(4):
                dma("sp", ogT[:, q4 * 8:(q4 + 1) * 8, :], ogT_d[q4 * 1024:(q4 + 1) * 1024, :].rearrange("(c p) t -> p c t", p=128),
                    [("ogT", 2 * q4), ("ogT", 2 * q4 + 1)], [(ob, q4)])
            wob = [P.alloc("wo0", 32 * 256 * 2), P.alloc("wo1", 32 * 256 * 2)]
            xrb = P.alloc("xr", 4 * 512 * 4)
            wov = w_out_c.rearrange("(c p) n -> p c n", p=128)

            def load_wo(nb):
                wt = view(wob[nb % 2], BF16, [32, 256])
                for q in range(2):
                    dma("pool", wt[:, q * 16:(q + 1) * 16, :], wov[:, q * 16:(q + 1) * 16, nb * 256:(nb + 1) * 256], (), [(wob[nb % 2], q)])
            load_wo(0)
            cnt2 = 0
            for nb in range(8):
                if nb + 1 < 8:
                    load_wo(nb + 1)
                wt = view(wob[nb % 2], BF16, [32, 256])
                for n2 in range(2):
                    n = nb * 2 + n2
                    for ti, (c0, w) in enumerate(TILES):
                        bank = next_bank()
                        for k in range(32):
                            mm(psum[:, bank, 0:w], wt[:, k, n2 * 128:(n2 + 1) * 128], ogT[:, k, c0:c0 + w], k == 0, k == 31,
                               [(wob[nb % 2], k // 16), (ob, k // 8)], [PS(bank)])
                        resid_update(xs, xs, n, c0, w, bank, mod_s[1][:, 2], 1, xrb, cnt2 % 4)
                        cnt2 += 1
            for b in wob + [xrb, ob]:
                P.free(b)
            ws_open()

        def final_norm():
            ws_close()
            xv = xs.rearrange("(c p) t -> p c t", p=128)
            yv = yT_out.rearrange("(c p) t -> p c t", p=128)
            NX, NQ = 4, 3
            xb_ = [P.alloc(f"fx{i}", 16 * 256 * 4) for i in range(NX)]
            sqb_ = [P.alloc(f"fs{i}", 16 * 256 * 4) for i in range(NQ)]
            rsb_ = [P.alloc(f"frs{i}", 3 * 256 * 4) for i in range(NQ)]
            tiles = [(i * 256, 256) for i in range(8)] + [(TP, 64)]
            banks = {}

            def bufs(si):
                c0, w = tiles[si]
                xb, sqb, rsb = xb_[si % NX], sqb_[si % NQ], rsb_[si % NQ]
                return (c0, w, xb, sqb, rsb, view(rsb, F32, [3, 256]), view(xb, F32, [16, 256])[:, :, 0:w], view(sqb, F32, [16, 256])[:, :, 0:w])

            def stage0(si):
                c0, w, xb, sqb, rsb, rs, xt, sq = bufs(si)
                dma("sp", xt, xv[:, :, c0:c0 + w], [("xs", n, (c0 // 512) * 512 if c0 < TP else TP) for n in range(16)], [(xb, 0)])

            def stage1(si):
                c0, w, xb, sqb, rsb, rs, xt, sq = bufs(si)
                act(sq, xt, AF.Square, [(xb, 0)], [(sqb, 0)])
                P.emit("dve", lambda e, o=rs[:, 0, 0:w], i=sq.rearrange("p c t -> p t c"): e.tensor_reduce(out=o, in_=i, axis=AX.X, op=ALU.add),
                       [(sqb, 0)], [(rsb, 0)])
                bank = next_bank()
                banks[si] = bank
                mm(psum[:, bank, 0:w], ones_f, rs[:, 0, 0:w], True, True, [CK, (rsb, 0)], [PS(bank)])

            def stage2(si):
                c0, w, xb, sqb, rsb, rs, xt, sq = bufs(si)
                bank = banks[si]
                act(rs[:, 1, 0:w], psum[:, bank, 0:w], AF.Sqrt, [PS(bank)], [(rsb, 1)], bias=eps_ap, scale=1.0 / D)
                P.emit("dve", lambda e, o=rs[:, 2, 0:w], i=rs[:, 1, 0:w]: e.reciprocal(out=o, in_=i), [(rsb, 1)], [(rsb, 2)])
                tt("pool", sq, xt, rs[:, 2, 0:w].unsqueeze(1).to_broadcast([128, 16, w]), ALU.mult, [(xb, 0), (rsb, 2)], [(sqb, 0)])

            def stage3(si):
                c0, w, xb, sqb, rsb, rs, xt, sq = bufs(si)
                tt("dve", sq, sq, normw_s[:, 4, :].unsqueeze(2).to_broadcast([128, 16, w]), ALU.mult, [(sqb, 0), CK], [(sqb, 0)])
                dma("sp", yv[:, :, c0:c0 + w], sq, [(sqb, 0)], [("yT", si)])
            n = len(tiles)
            for si in range(min(2, n)):
                stage0(si)
            for si in range(n + 2):
                if si + 2 < n:
                    stage0(si + 2)
                if si < n:
                    stage1(si)
                if 1 <= si <= n:
                    stage2(si - 1)
                if si >= 2:
                    stage3(si - 2)
            for b in xb_ + sqb_ + rsb_:
                P.free(b)
            ws_open()

        ada_all()
        layer0_mixer()
        ffn(0)
        layer1_mixer()
        ffn(1)
        final_norm()

        print("arena peak KB", P.peak / 1024, "ops", len(P.ops), {e: len(v) for e, v in P.by_eng.items()})
        P.finalize(nc)
    return nc


POOL_WINDOWS = (2, 4, 8, 16)


def _fm(v):
    v = np.asarray(v, np.float32)
    return np.ascontiguousarray(v.reshape(-1, 128).T)


def _core_inputs(c, I, shared):
    b, j = c // 4, c % 4
    f32 = np.float32
    m = dict(shared)
    xp = I["x_prompt"][b, j * TP:(j + 1) * TP]
    xsamp = I["x_sample"][2 * c:2 * c + 2].reshape(64, D)
    m["xT"] = np.ascontiguousarray(np.concatenate([xp, xsamp], 0).T)
    xh = np.zeros((16, D), f32)
    if j > 0:
        xh[1:] = I["x_prompt"][b, j * TP - 15:j * TP]
    m["xhT"] = np.ascontiguousarray(xh.T)
    c3 = np.stack([I["c_prompt"][b], I["c_sample"][2 * c], I["c_sample"][2 * c + 1]], -1)
    m["c3"] = np.ascontiguousarray(c3.reshape(16, 128, 3).transpose(1, 0, 2))
    cf = np.zeros((128, 8), f32)
    cf[:, 0] = 1.0 if j > 0 else 0.0
    m["cflag"] = cf
    ic = np.zeros((128, 4, 16), f32)
    for g, w in enumerate(POOL_WINDOWS):
        pos = np.arange(16) + j * TP
        ic[:, g, :] = 1.0 / np.minimum(w, pos + 1)
    m["icnt"] = ic
    ph = np.zeros((128, 8, 2, 16), f32)
    for s in range(2):
        h = I["state_b_pool"][0, 2 * c + s]
        ph[:, :, s, 1:] = h.T.reshape(8, 128, 15).transpose(1, 0, 2)
    m["phist"] = ph
    m["sret"] = np.ascontiguousarray(I["state_c_ret"][0, 2 * c:2 * c + 2], f32)
    m["w_ada_sh"] = np.ascontiguousarray(I["w_ada"][:, :, c * 1536:(c + 1) * 1536], f32)
    m["bada_rows"] = np.ascontiguousarray(np.broadcast_to(np.asarray(I["b_ada"], f32)[None, :, c * 1536:(c + 1) * 1536], (18, 2, 1536)))
    sel = np.zeros((18, 4), f32)
    sel[b, 0] = 1.0
    sel[2 + 2 * c, 1] = 1.0
    sel[2 + 2 * c + 1, 2] = 1.0
    m["selT"] = sel
    pos = np.concatenate([j * TP + np.arange(TP), 2048 + np.arange(32), 2048 + np.arange(32)]).astype(np.float32)
    freq = (np.float32(10000.0) ** (-np.arange(128, dtype=np.float32) / np.float32(128))).astype(np.float32)
    ang = (pos[None, :] * freq[:, None]).astype(np.float32)
    m["cosT"] = np.cos(ang).astype(f32)
    m["sinT"] = np.sin(ang).astype(f32)
    gam = 1.0 - 2.0 ** (-5.0 - np.arange(8, dtype=np.float64))
    cf = np.zeros((128, 64), np.float64)
    for r in range(8):
        rb, rj = r // 4, r % 4
        if rb == b and rj < j:
            cf[:, r * 8:(r + 1) * 8] = gam[None, :] ** (2048.0 * (j - 1 - rj))
    m["coef"] = cf.astype(f32)
    return m


def _shared_inputs(I):
    f32 = np.float32
    m = {}
    c18 = np.concatenate([np.asarray(I["c_prompt"], f32), np.asarray(I["c_sample"], f32)], 0)
    m["c18T"] = np.ascontiguousarray(c18.reshape(18, 16, 128).transpose(2, 1, 0))
    nw = np.stack([I["norm_mix"][0], I["norm_mix"][1], I["norm_ffn"][0], I["norm_ffn"][1], I["norm_final"]], 0)
    m["normw"] = np.ascontiguousarray(np.asarray(nw, f32).reshape(5, 16, 128).transpose(2, 0, 1))
    m["w_in_ab"] = np.ascontiguousarray(I["w_in_ab"][0], f32)
    m["lng"] = np.ascontiguousarray(np.broadcast_to(np.asarray(I["ln_v_g"][0], f32), (128, 1024)))
    m["lnb"] = np.ascontiguousarray(np.broadcast_to(np.asarray(I["ln_v_b"][0], f32), (128, 1024)))
    ws = np.asarray(I["w_s"][0], f32)
    m["wsT"] = np.ascontiguousarray(ws.transpose(2, 0, 1))
    wss = np.zeros((64, 8, 64), f32)
    blk = ws[:, :32, :32].transpose(2, 0, 1)
    wss[0:32, :, 0:32] = blk
    wss[32:64, :, 32:64] = blk
    m["wsTs"] = wss
    bs = np.asarray(I["b_s"][0], f32)
    m["bsbc"] = np.ascontiguousarray(np.broadcast_to(bs, (128, 8, 128)))
    m["bsbcs"] = np.ascontiguousarray(np.broadcast_to(np.concatenate([bs[:, :32], bs[:, :32]], 1), (128, 8, 64)))
    m["w_pool"] = np.ascontiguousarray(I["w_pool"][0], f32)
    m["pscaleT"] = _fm(I["pool_scale"][0])
    m["w_out_ab"] = np.ascontiguousarray(I["w_out_ab"][0], f32)
    m["w_ffn_gu"] = np.ascontiguousarray(I["w_ffn_gu"], f32)
    m["w_ffn_down"] = np.ascontiguousarray(I["w_ffn_down"], f32)
    m["ident"] = np.eye(128, dtype=f32)
    m["w_in_c"] = np.ascontiguousarray(I["w_in_c"][0], f32)
    m["w_out_c"] = np.ascontiguousarray(I["w_out_c"][0], f32)
    gam = 1.0 - 2.0 ** (-5.0 - np.arange(8, dtype=np.float64))
    p = np.arange(128, dtype=np.float64)
    dt_ = np.zeros((128, 32), np.float64)
    dt_[:, 0:8] = gam[None, :] ** (p[:, None] + 1.0)
    dt_[:, 8:16] = gam[None, :] ** (127.0 - p[:, None])
    dt_[:, 16:24] = gam[None, :] ** (31.0 - (p[:, None] % 32))
    m["dtab"] = dt_.astype(f32)
    d2 = np.zeros((128, 128), np.float64)
    for bi in range(16):
        d2[:, bi * 8:(bi + 1) * 8] = gam[None, :] ** (2047.0 - (bi * 128.0 + p[:, None]))
    m["dtab2"] = d2.astype(f32)
    diff = p[None, :] - p[:, None]
    dec = np.where(diff[:, None, :] >= 0, gam[None, :, None] ** np.maximum(diff[:, None, :], 0.0), 0.0)
    m["decT"] = np.ascontiguousarray(dec.astype(f32))
    return m


_NC_CACHE = {}


def _run(inputs, debug=False):
    I = {k: np.asarray(v) for k, v in inputs.items()}
    if debug not in _NC_CACHE:
        _NC_CACHE[debug] = build_program(debug)
    nc = _NC_CACHE[debug]
    shared = _shared_inputs(I)
    in_maps = [_core_inputs(c, I, shared) for c in range(NCORES)]
    res = run_bass_kernel_spmd(nc, in_maps, core_ids=list(range(NCORES)))
    return res.results


def kernel(**inputs):
    R = _run(inputs)
    f32 = np.float32
    y_prompt = np.zeros((2, 8192, D), f32)
    y_sample = np.zeros((16, 32, D), f32)
    pool_p = np.zeros((1, 2, 15, 1024), f32)
    pool_s = np.zeros((1, 16, 15, 1024), f32)
    v_s = np.zeros((1, 16, 32, 1024), f32)
    ret_p = np.zeros((1, 2, 8, 256, 512), f32)
    ret_s = np.zeros((1, 16, 8, 256, 512), f32)
    for c in range(NCORES):
        b, j = c // 4, c % 4
        r = R[c]
        yT = r["yT"]
        y_prompt[b, j * TP:(j + 1) * TP] = yT[:, :TP].T
        y_sample[2 * c] = yT[:, TP:TP + 32].T
        y_sample[2 * c + 1] = yT[:, TP + 32:].T
        po = r["pool_o"]
        po = po.transpose(2, 3, 1, 0).reshape(3, 16, 1024)
        if j == 3:
            pool_p[0, b] = po[0, 1:]
        pool_s[0, 2 * c] = po[1, 1:]
        pool_s[0, 2 * c + 1] = po[2, 1:]
        v_s[0, 2 * c] = r["vs_o"][:32]
        v_s[0, 2 * c + 1] = r["vs_o"][32:]
        if j == 3:
            ret_p[0, b] = r["retp_o"]
        ret_s[0, 2 * c:2 * c + 2] = r["rets_o"]
    return (y_prompt, y_sample, pool_p, pool_s, v_s, ret_p, ret_s)
```
